# Optimizing a Trainium2 kernel written in Bass

```python
import math
import jax, jax.numpy as jnp
from jax import lax
import numpy as np

D_MODEL = 1024
BATCH = 8
SEQ = 2048
DEPTH = 1
DEC_BATCH = 128
DEC_SEQ = 1
PAST_LEN = 16384
PAGE_SIZE = 128

EPS = 1e-6
D_LRU = D_MODEL
LRU_BLOCKS = 8
LRU_BW = D_LRU // LRU_BLOCKS
LRU_C = 8.0
CONV_W = 4
GLA_HEADS = 4
GLA_DV = D_MODEL // GLA_HEADS
GLA_DK = GLA_DV // 2
GLA_RANK = 16
GLA_TAU = 16.0
GLA_CHUNK = 64
D_GLA = GLA_HEADS * GLA_DV
D_MIX = D_LRU + D_GLA
IN_SIZES = (D_LRU, D_LRU, GLA_HEADS * GLA_DK, GLA_HEADS * GLA_DK, D_GLA, GLA_RANK, D_GLA)
D_IN = sum(IN_SIZES)
SPLITS = [int(s) for s in np.cumsum(IN_SIZES)[:-1]]

kernel_name = "hymba_rglru_gla_adaln_step"


def _rmsnorm(x, g):
    xf = x.astype(jnp.float32)
    y = xf * lax.rsqrt(jnp.mean(xf * xf, axis=-1, keepdims=True) + EPS)
    return (y * g.astype(jnp.float32)).astype(x.dtype)


def _lin_comb(left, right):
    a1, b1 = left
    a2, b2 = right
    return a1 * a2, a2 * b1 + b2


def _gla_chunked(q, k, v, g, S0):
    Bn, T, H, K = q.shape
    V = v.shape[-1]
    C = min(GLA_CHUNK, T)
    n = -(-T // C)
    pad = n * C - T
    padw = ((0, 0), (0, pad), (0, 0), (0, 0))
    q, k, v, g = (jnp.pad(t, padw) for t in (q, k, v, g))
    def blk(t):
        return t.reshape(Bn, n, C, H, t.shape[-1]).transpose(1, 0, 3, 2, 4)
    q, k, v, g = blk(q), blk(k), blk(v), blk(g)
    b = jnp.cumsum(g, axis=3)
    b_last = b[:, :, :, -1:, :]
    q_in = q * jnp.exp(b)
    k_in = k * jnp.exp(-b)
    k_dec = k * jnp.exp(b_last - b)
    mask = jnp.tril(jnp.ones((C, C), dtype=bool))
    A = jnp.einsum('nbhck,nbhsk->nbhcs', q_in, k_in)
    A = jnp.where(mask, A, 0.0)
    o_intra = jnp.einsum('nbhcs,nbhsv->nbhcv', A, v)

    def step(S, xs):
        qn, kn, vn, dl = xs
        o = jnp.einsum('bhck,bhkv->bhcv', qn, S)
        S = jnp.exp(dl[:, :, 0, :])[..., None] * S + jnp.einsum('bhck,bhcv->bhkv', kn, vn)
        return S, o

    S_fin, o_inter = lax.scan(step, S0, (q_in, k_dec, v, b_last))
    o = (o_intra + o_inter).transpose(1, 0, 3, 2, 4).reshape(Bn, n * C, H, V)[:, :T]
    return o, S_fin


def _layer(x, c, h0, conv0, S0, g_norm, w_ada, b_ada, w_in, conv_w, conv_b,
           w_gate_x, b_gate_x, w_gate_a, b_gate_a, lru_lambda,
           w_gla_g2, b_gla_g2, g_gla_norm, w_out):
    Bn, T, _ = x.shape
    f32 = jnp.float32
    ada = jax.nn.silu(c) @ w_ada + b_ada
    shift, scale, gate = jnp.split(ada, 3, axis=-1)
    hn = _rmsnorm(x, g_norm) * (1.0 + scale[:, None]) + shift[:, None]
    z = hn @ w_in
    xa, za, q, k, v, glr, zg = jnp.split(z, SPLITS, axis=-1)

    xp = jnp.concatenate([conv0.astype(xa.dtype), xa], axis=1)
    xc = conv_b + sum(xp[:, j:j + T] * conv_w[j] for j in range(CONV_W))
    new_conv = xp[:, T:]
    xb = xc.reshape(Bn, T, LRU_BLOCKS, LRU_BW)
    gx = jax.nn.sigmoid(jnp.einsum('btnd,nde->btne', xb, w_gate_x).reshape(Bn, T, D_LRU) + b_gate_x)
    ga = jax.nn.sigmoid(jnp.einsum('btnd,nde->btne', xb, w_gate_a).reshape(Bn, T, D_LRU) + b_gate_a)
    log_a = (LRU_C * ga.astype(f32)) * jax.nn.log_sigmoid(lru_lambda.astype(f32))
    a = jnp.exp(log_a)
    mult = jnp.sqrt(-jnp.expm1(2.0 * log_a))
    u = mult * (gx * xc).astype(f32)
    u = u.at[:, 0].add(a[:, 0] * h0.astype(f32))
    _, h = lax.associative_scan(_lin_comb, (a, u), axis=1)
    y_lru = h.astype(x.dtype) * jax.nn.silu(za)

    qh = q.reshape(Bn, T, GLA_HEADS, GLA_DK).astype(f32) * (GLA_DK ** -0.5)
    kh = k.reshape(Bn, T, GLA_HEADS, GLA_DK).astype(f32)
    vh = v.reshape(Bn, T, GLA_HEADS, GLA_DV).astype(f32)
    log_alpha = jax.nn.log_sigmoid((glr @ w_gla_g2 + b_gla_g2).astype(f32)) / GLA_TAU
    log_alpha = log_alpha.reshape(Bn, T, GLA_HEADS, GLA_DK)
    o, S_new = _gla_chunked(qh, kh, vh, log_alpha, S0.astype(f32))
    o = o * lax.rsqrt(jnp.mean(o * o, axis=-1, keepdims=True) + EPS) * g_gla_norm.astype(f32)
    y_gla = o.reshape(Bn, T, D_GLA).astype(x.dtype) * jax.nn.silu(zg)

    mix = jnp.concatenate([y_lru, y_gla], axis=-1) @ w_out
    x = x + gate[:, None] * mix
    return x, h[:, -1], new_conv, S_new


def setup_inputs(seed: int = 0) -> dict:
    key = jax.random.key(seed)
    ks = jax.random.split(key, 24)
    nrm = lambda k, s, sc: jax.random.normal(k, s, jnp.float32) * sc
    u = jax.random.uniform(ks[14], (DEPTH, D_LRU), jnp.float32, 0.9, 0.999)
    s = u ** (1.0 / LRU_C)
    lru_lambda = jnp.log(s) - jnp.log1p(-s)
    return {
        "x_prompt": nrm(ks[0], (BATCH, SEQ, D_MODEL), 1.0),
        "x_sample": nrm(ks[1], (DEC_BATCH, DEC_SEQ, D_MODEL), 1.0),
        "state_lru_h": nrm(ks[2], (DEPTH, DEC_BATCH, D_LRU), 0.5),
        "state_lru_conv": nrm(ks[3], (DEPTH, DEC_BATCH, CONV_W - 1, D_LRU), 1.0),
        "state_gla": nrm(ks[4], (DEPTH, DEC_BATCH, GLA_HEADS, GLA_DK, GLA_DV), 1.0),
        "c_prompt": nrm(ks[5], (BATCH, D_MODEL), 1.0),
        "c_sample": nrm(ks[6], (DEC_BATCH, D_MODEL), 1.0),
        "g_norm": 1.0 + nrm(ks[7], (DEPTH, D_MODEL), 0.02),
        "w_ada": nrm(ks[8], (DEPTH, D_MODEL, 3 * D_MODEL), 0.5 * D_MODEL ** -0.5),
        "b_ada": nrm(ks[9], (DEPTH, 3 * D_MODEL), 0.02),
        "w_in": nrm(ks[10], (DEPTH, D_MODEL, D_IN), D_MODEL ** -0.5),
        "conv_w": nrm(ks[11], (DEPTH, CONV_W, D_LRU), CONV_W ** -0.5),
        "conv_b": nrm(ks[12], (DEPTH, D_LRU), 0.02),
        "w_gate_x": nrm(ks[13], (DEPTH, LRU_BLOCKS, LRU_BW, LRU_BW), LRU_BW ** -0.5),
        "b_gate_x": nrm(ks[15], (DEPTH, D_LRU), 0.02),
        "w_gate_a": nrm(ks[16], (DEPTH, LRU_BLOCKS, LRU_BW, LRU_BW), LRU_BW ** -0.5),
        "b_gate_a": nrm(ks[17], (DEPTH, D_LRU), 0.02),
        "lru_lambda": lru_lambda,
        "w_gla_g2": nrm(ks[18], (DEPTH, GLA_RANK, GLA_HEADS * GLA_DK), GLA_RANK ** -0.5),
        "b_gla_g2": nrm(ks[19], (DEPTH, GLA_HEADS * GLA_DK), 0.02),
        "g_gla_norm": 1.0 + nrm(ks[20], (DEPTH, GLA_HEADS, GLA_DV), 0.02),
        "w_out": nrm(ks[21], (DEPTH, D_MIX, D_MODEL), D_MIX ** -0.5),
        "g_final": 1.0 + nrm(ks[22], (D_MODEL,), 0.02),
    }


def reference(x_prompt, x_sample, state_lru_h, state_lru_conv, state_gla, c_prompt, c_sample,
              g_norm, w_ada, b_ada, w_in, conv_w, conv_b, w_gate_x, b_gate_x, w_gate_a, b_gate_a,
              lru_lambda, w_gla_g2, b_gla_g2, g_gla_norm, w_out, g_final):
    bp = x_prompt.shape[0]
    yp, ys = x_prompt, x_sample
    hp_l, cp_l, sp_l, hs_l, cs_l, ss_l = [], [], [], [], [], []
    for l in range(DEPTH):
        w = (g_norm[l], w_ada[l], b_ada[l], w_in[l], conv_w[l], conv_b[l],
             w_gate_x[l], b_gate_x[l], w_gate_a[l], b_gate_a[l], lru_lambda[l],
             w_gla_g2[l], b_gla_g2[l], g_gla_norm[l], w_out[l])
        h0 = jnp.zeros((bp, D_LRU), jnp.float32)
        conv0 = jnp.zeros((bp, CONV_W - 1, D_LRU), x_prompt.dtype)
        S0 = jnp.zeros((bp, GLA_HEADS, GLA_DK, GLA_DV), jnp.float32)
        yp, hp, cp, sp = _layer(yp, c_prompt, h0, conv0, S0, *w)
        ys, hs, cs, ss = _layer(ys, c_sample, state_lru_h[l], state_lru_conv[l], state_gla[l], *w)
        hp_l.append(hp); cp_l.append(cp); sp_l.append(sp)
        hs_l.append(hs); cs_l.append(cs); ss_l.append(ss)
    yp = _rmsnorm(yp, g_final)
    ys = _rmsnorm(ys, g_final)
    return (yp, ys, jnp.stack(hp_l), jnp.stack(cp_l), jnp.stack(sp_l),
            jnp.stack(hs_l), jnp.stack(cs_l), jnp.stack(ss_l))
```

```python
import math
from contextlib import ExitStack

import numpy as np
import concourse.bass as bass
import concourse.mybir as mybir
from concourse.bass_utils import run_bass_kernel_spmd

F32 = mybir.dt.float32
F32R = mybir.dt.float32r
BF16 = mybir.dt.bfloat16
AF = mybir.ActivationFunctionType
ALU = mybir.AluOpType

COMPUTE = ("pe", "act", "dve", "pool")
ENGS = {"pe": "tensor", "act": "scalar", "dve": "vector", "pool": "gpsimd", "sp": "sync"}

D = 1024
T = 2048
TB = 256
NB = T // TB
DIN = 5136
XA, ZA, QO, KO, VO, GLR, ZG = 0, 1024, 2048, 2560, 3072, 4096, 4112
NS = 16
NTOK = 17
EPS = 1e-6
WIN_GROUPS = [(i * 512, (i + 1) * 512) for i in range(8)] + [(4096, 4624), (4624, 5136)]


class Op:
    __slots__ = ("idx", "eng", "fn", "kind", "cost", "lat", "tset", "semkey", "preds", "succs", "nun", "ready", "count", "start", "finish", "dma", "nbytes")


class Prog:
    def __init__(self, nc):
        self.nc = nc
        self.ops = []
        self.res = {}
        self.cnt = {e: 0 for e in COMPUTE}
        self.dcnt = {}
        self.waited = {e: {} for e in ENGS}
        self.keep = ExitStack()
        self.sems = {}
        self.bank_i = 0
        self.barrier_vals = None
        self.act_set = "L"
        self.sim_log = []

    def sb(self, stack, name, shape, dtype):
        return stack.enter_context(self.nc.sbuf_tensor("sb_" + name, list(shape), dtype))

    def _add(self, o, reads, writes):
        o.idx = len(self.ops)
        preds = set()
        for r in reads:
            st = self.res.get(r)
            if st and st[0] is not None:
                preds.add(st[0])
        for w in writes:
            st = self.res.get(w)
            if st:
                if st[0] is not None:
                    preds.add(st[0])
                preds.update(st[1])
        preds.discard(o)
        o.preds = sorted(preds, key=lambda p: p.idx)
        o.succs = []
        for p in o.preds:
            p.succs.append(o)
        for r in reads:
            st = self.res.setdefault(r, [None, []])
            st[1].append(o)
        for w in writes:
            self.res[w] = [o, []]
        self.ops.append(o)

    def op(self, eng, fn, reads=(), writes=(), cost=0.3, tset=None):
        o = Op()
        o.eng = eng; o.fn = fn; o.kind = "op"; o.cost = cost; o.lat = 0.0; o.tset = tset; o.semkey = eng; o.dma = None
        self._add(o, reads, writes)

    def dma(self, q, semkey, out, in_, reads=(), writes=(), nbytes=0):
        o = Op()
        o.eng = q; o.fn = None; o.kind = "dma"; o.cost = 0.07; o.lat = 2.0; o.tset = None
        o.nbytes = nbytes
        o.semkey = ("dma", semkey); o.dma = (out, in_)
        self._add(o, reads, writes)

    def _sem(self, key):
        if key not in self.sems:
            nm = ("s_" + key) if isinstance(key, str) else ("d_" + key[1])
            self.sems[key] = self.keep.enter_context(self.nc.semaphore(nm))
        return self.sems[key]

    def barrier(self):
        self.flush()
        self.barrier_vals = (dict(self.cnt), dict(self.dcnt))
        self.res.clear()

    def _schedule(self):
        ops = self.ops
        import os
        if os.environ.get("SCHED", "1") == "0":
            order = {e: [] for e in ENGS}
            for o in ops:
                order[o.eng].append(o)
            self.sim_time = 0.0
            return order
        ready = {e: [] for e in ENGS}
        free = {e: 0.0 for e in ENGS}
        order = {e: [] for e in ENGS}
        for o in ops:
            o.nun = len(o.preds)
            o.ready = 0.0
            if o.nun == 0:
                ready[o.eng].append(o)
        cur = self.act_set
        left = len(ops)
        dma_free = {e: 0.0 for e in ENGS}
        import os
        BLL = float(os.environ.get("SCH_BLL", "0.2"))
        bl = {}
        for o in reversed(ops):
            m = 0.0
            for s_ in o.succs:
                v = bl[s_] + (BLL if s_.eng != o.eng else 0.0)
                if v > m:
                    m = v
            bl[o] = m + o.cost + o.lat
        WIN = float(os.environ.get("SCH_WIN", "0.3"))
        LAT = float(os.environ.get("SCH_LAT", "0.45"))
        SEED = int(os.environ.get("SCH_SEED", "0"))
        if SEED:
            rng = np.random.default_rng(SEED)
            jit = rng.uniform(0.95, 1.05, size=len(ops))
            for o, jv in zip(ops, jit):
                o.cost *= float(jv)
        PEN = float(os.environ.get("SCH_PEN", "1.3"))
        OVH = float(os.environ.get("SCH_OVH", "0.0"))
        while left:
            best = None
            for e in ENGS:
                fe = free[e]
                cands = []
                mst = None
                for o in ready[e]:
                    st = o.ready if o.ready > fe else fe
                    if e == "act" and o.tset is not None and o.tset != cur:
                        st += PEN
                    cands.append((st, o))
                    if mst is None or st < mst:
                        mst = st
                if not cands:
                    continue
                sel = None
                for st, o in cands:
                    if st <= mst + WIN:
                        k2 = (-bl[o], o.idx)
                        if sel is None or k2 < sel[0]:
                            sel = (k2, st, o)
                key = (mst, sel[2].idx)
                if best is None or key < best[0]:
                    best = (key, sel[2], sel[1])
            _, o, st = best
            e = o.eng
            ready[e].remove(o)
            if e == "act" and o.tset is not None:
                cur = o.tset
            o.start = st
            free[e] = st + o.cost + OVH
            if o.kind == "dma":
                x0 = max(st + o.cost, dma_free[e])
                dma_free[e] = x0 + o.nbytes / (330e3 if e == "pool" else 250e3)
                o.finish = dma_free[e] + o.lat
            else:
                o.finish = st + o.cost
            order[e].append(o)
            left -= 1
            for s_ in o.succs:
                fin = o.finish if s_.eng == e else o.finish + LAT
                if fin > s_.ready:
                    s_.ready = fin
                s_.nun -= 1
                if s_.nun == 0:
                    ready[s_.eng].append(s_)
        self.act_set = cur
        self.sim_time = max(free.values())
        busy = {e: 0.0 for e in ENGS}
        for o in ops:
            busy[o.eng] += o.cost
        self.sim_log.append((len(ops), round(self.sim_time, 1), {e: round(v, 1) for e, v in busy.items()}))
        return order

    def flush(self):
        nc = self.nc
        if not self.ops and self.barrier_vals is None:
            return
        order = self._schedule()
        streams = {e: [] for e in ENGS}
        if self.barrier_vals is not None:
            cv, dv = self.barrier_vals
            for e in ENGS:
                for k, v in cv.items():
                    if k != e and v > self.waited[e].get(k, 0):
                        self.waited[e][k] = v
                        streams[e].append(("wait", k, v))
                for k, v in dv.items():
                    kk = ("dma", k)
                    if v > self.waited[e].get(kk, 0):
                        self.waited[e][kk] = v
                        streams[e].append(("wait", kk, v))
            self.barrier_vals = None
        for e in ENGS:
            for o in order[e]:
                if o.kind == "op":
                    self.cnt[e] += 1
                    o.count = self.cnt[e]
                else:
                    k = o.semkey[1]
                    self.dcnt[k] = self.dcnt.get(k, 0) + 16
                    o.count = self.dcnt[k]
        for e in ENGS:
            w = self.waited[e]
            for o in order[e]:
                need = {}
                for p in o.preds:
                    if p.kind == "op" and p.eng == "pe" and e == "pe":
                        continue
                    if w.get(p.semkey, 0) >= p.count:
                        continue
                    if p.count > need.get(p.semkey, 0):
                        need[p.semkey] = p.count
                for k, v in need.items():
                    w[k] = v
                    streams[e].append(("wait", k, v))
                streams[e].append(("x", o))
        for st in streams.values():
            for it in st:
                if it[0] == "wait":
                    self._sem(it[1])
                elif it[1].kind == "dma":
                    self._sem(it[1].semkey)
        for e in COMPUTE:
            self._sem(e)

        def replay(name, eng):
            for it in streams[name]:
                if it[0] == "wait":
                    eng.wait_ge(self._sem(it[1]), it[2])
                else:
                    o = it[1]
                    if o.kind == "op":
                        o.fn(eng).then_inc(self._sem(name), 1)
                    else:
                        eng.dma_start(out=o.dma[0], in_=o.dma[1]).then_inc(self._sem(o.semkey), 16)

        with nc.Block() as block:
            for name, attr in ENGS.items():
                if not streams[name]:
                    continue
                getattr(block, attr)(lambda eng, name=name: replay(name, eng))
        self.ops = []


def win_keys(c0, c1):
    return [("win", g) for g, (a, b) in enumerate(WIN_GROUPS) if a < c1 and c0 < b]


def build_nc():
    nc = bass.Bass("TRN2", target_bir_lowering=False)
    P = Prog(nc)
    keep = P.keep

    def din(name, shape):
        return nc.dram_tensor(name, list(shape), F32, kind="ExternalInput").ap()

    def dout(name, shape):
        return nc.dram_tensor(name, list(shape), F32, kind="ExternalOutput").ap()

    xp = din("xp", [T, D]); xs = din("xs", [NS, D]); cT = din("cT", [D, NTOK])
    w_ada = din("w_ada", [D, 3 * D]); b_ada_tok = din("b_ada_tok", [NTOK, 3 * D]); b_adaT = din("b_adaT", [128, 16])
    g_normT = din("g_normT", [128, 8]); g_norm_bc = din("g_norm_bc", [NS, D])
    w_in = din("w_in", [D, DIN]); cw_d = din("cw", [128, 8, 4]); pvec_d = din("pvec", [128, 8, 4])
    wgx_d = din("wgx", [128, 8, 128]); wga_d = din("wga", [128, 8, 128])
    w_g2 = din("w_g2", [16, 512]); b_g2T = din("b_g2T", [128, 4]); g_glaT = din("g_glaT", [128, 8])
    w_out = din("w_out", [2 * D, D]); g_final_bc = din("g_final_bc", [128, D])
    h0T_d = din("h0T", [128, 8, NS]); c0T_d = din("c0T", [128, 8, 3, NS]); S0_d = din("S0", [NS, 4, 128, 256])
    ident_d = din("ident", [128, 128]); maskT_d = din("maskT", [128, 128]); sel_d = din("sel", [NTOK, 128])
    eye16_d = din("eye16", [128, NS * NS])

    yp = dout("yp", [T, D]); ys = dout("ys", [NS, D]); hpT = dout("hpT", [128, 8]); cpT = dout("cpT", [128, 8, 3])
    sp_o = dout("sp", [4, 128, 256]); hsT = dout("hsT", [128, 8, NS]); csT = dout("csT", [128, 8, 3, NS])
    ss_o = dout("ss", [NS, 4, 128, 256])
    out_dma_keys = []

    w_in_bf = P.sb(keep, "w_in_bf", [128, 8, DIN], BF16)
    w_out_bf = P.sb(keep, "w_out_bf", [128, 16, D], BF16)
    wgx = P.sb(keep, "wgx_bf", [128, 8, 128], BF16)
    wga = P.sb(keep, "wga_bf", [128, 8, 128], BF16)
    wg2 = P.sb(keep, "wg2_bf", [16, 512], BF16)
    hgT = P.sb(keep, "hgT", [128, 8], F32)
    gfin = P.sb(keep, "gfin", [128, D], F32)
    ident = P.sb(keep, "ident_bf", [128, 128], BF16)
    maskT = P.sb(keep, "maskT", [128, 128], F32)
    ones = P.sb(keep, "ones", [128, 128], F32)
    lnh = P.sb(keep, "lnh", [128, 1], F32)
    cw = P.sb(keep, "cw", [128, 8, 4], F32)
    pvec = P.sb(keep, "pvec", [128, 8, 4], F32)
    hbg = P.sb(keep, "hbg", [128, 8, 2], F32)
    c8 = P.sb(keep, "c8", [128, 8], F32)
    hc8 = P.sb(keep, "hc8", [128, 8], F32)
    geffp = P.sb(keep, "geffp", [128, 8], F32)
    shiftp = P.sb(keep, "shiftp", [128, 8], F32)
    nbg2 = P.sb(keep, "nbg2", [128, 4], F32)
    hist = P.sb(keep, "hist", [128, 8, 3], F32)
    hcarry = P.sb(keep, "hcarry", [128, 8], F32)
    S_f = P.sb(keep, "S_f", [128, 4, 256], F32)
    S_b = P.sb(keep, "S_b", [128, 4, 256], BF16)
    ygla = P.sb(keep, "ygla", [128, D], BF16)
    banks = [keep.enter_context(nc.psum_tensor(f"bank{i}", [128, 512], F32)) for i in range(8)]

    reserved = set()

    def bank(reserve=False):
        while P.bank_i % 8 in reserved:
            P.bank_i += 1
        i = P.bank_i % 8
        P.bank_i += 1
        if reserve:
            reserved.add(i)
        return ("ps", i), banks[i]

    def nfree(ap):
        n = 1
        for d in ap.shape[1:]:
            n *= d
        return n

    def act(out, in_, func, reads, writes, scale=1.0, bias=None, accum=None):
        kw = {}
        if bias is not None:
            kw["bias"] = bias
        if accum is not None:
            kw["accum_out"] = accum
        tset = "T" if func == AF.Tanh else ("L" if func == AF.Ln else None)
        P.op("act", lambda a: a.activation(out=out, in_=in_, func=func, scale=scale, **kw), reads, writes,
             cost=0.2 + nfree(out) / 1200.0 + (0.1 if accum is not None else 0.0), tset=tset)

    def ecost(eng, n, k=1.0):
        if eng == "pool":
            return 0.25 + n / 330.0
        return 0.15 + k * n / 960.0

    def tt(eng, out, in0, in1, op, reads, writes):
        P.op(eng, lambda v: v.tensor_tensor(out=out, in0=in0, in1=in1, op=op), reads, writes, cost=ecost(eng, nfree(out)))

    def ts(eng, out, in0, s1, s2, op0, op1, reads, writes):
        c = ecost(eng, nfree(out), 0.7)
        if s2 is None:
            P.op(eng, lambda v: v.tensor_scalar(out=out, in0=in0, scalar1=s1, scalar2=None, op0=op0), reads, writes, cost=c)
        else:
            P.op(eng, lambda v: v.tensor_scalar(out=out, in0=in0, scalar1=s1, scalar2=s2, op0=op0, op1=op1), reads, writes, cost=c)

    def stt(out, in0, scalar, in1, op0, op1, reads, writes):
        P.op("dve", lambda v: v.scalar_tensor_tensor(out=out, in0=in0, scalar=scalar, in1=in1, op0=op0, op1=op1), reads, writes,
             cost=ecost("dve", nfree(out)))

    def scan(out, d0, d1, initial, reads, writes):
        P.op("dve", lambda v: v.tensor_tensor_scan(out=out, data0=d0, data1=d1, initial=initial, op0=ALU.mult, op1=ALU.add), reads, writes,
             cost=0.15 + 2.0 * nfree(out) / 960.0)

    def cp(eng, out, in_, reads, writes):
        if eng == "act":
            act(out, in_, AF.Copy, reads, writes)
        else:
            P.op(eng, lambda v: v.tensor_copy(out=out, in_=in_), reads, writes, cost=ecost(eng, nfree(out), 0.7))

    def mmg(out, pairs, reads, writes, first_start=True, **mkw):
        n = len(pairs)
        ncol = max(nfree(out), 64)

        def fn(pe):
            ins = None
            for i, (l, r) in enumerate(pairs):
                ins = pe.matmul(out, lhsT=l, rhs=r, start=(first_start and i == 0), stop=(i == n - 1), **mkw)
            return ins
        c4 = 4.0 if pairs[0][0].dtype == F32 else 1.0
        P.op("pe", fn, reads, writes, cost=n * (c4 * ncol / 2200.0 + 0.027))

    def transposes(outs_ins, idn, reads, writes):
        def fn(pe):
            ins = None
            for o, s_ in outs_ins:
                ins = pe.transpose(out=o, in_=s_, identity=idn)
            return ins
        P.op("pe", fn, reads, writes, cost=0.07 * len(outs_ins))

    def rstd_from(ssq, lnv, rstd, n, key_ss, key_ln, key_r):
        kss = key_ss if isinstance(key_ss, list) else [key_ss]
        act(lnv, ssq, AF.Ln, kss, [key_ln], scale=1.0 / n, bias=EPS)
        act(rstd, lnv, AF.Exp, [key_ln], [key_r], scale=-0.5)

    A = ExitStack()
    identf = P.sb(A, "ident_f", [128, 128], F32)
    P.dma("sp", "cst0", identf[:], ident_d[:, :], writes=["identf"])
    P.dma("sp", "cst1", maskT[:], maskT_d[:, :], writes=["maskT"])
    P.dma("sp", "cst2", cw[:], cw_d[:, :, :], writes=["cw"])
    P.dma("sp", "cst3", pvec[:], pvec_d[:, :, :], writes=["pvec"])
    P.dma("sp", "cst4", hgT[:], g_glaT[:, :], writes=["hgT"])
    P.dma("sp", "cst5", gfin[:], g_final_bc[:, :], writes=["gfin"])
    P.dma("sp", "cst6", nbg2[:], b_g2T[:, :], writes=["nbg2"])
    P.dma("pool", "cst7", wgx[:], wgx_d[:, :, :], writes=["wgx"])
    P.dma("pool", "cst8", wga[:], wga_d[:, :, :], writes=["wga"])
    P.dma("pool", "cst9", wg2[:], w_g2[:, :], writes=["wg2"])
    cp("pool", ident[:], identf[:], ["identf"], ["ident"])
    P.op("pool", lambda g: g.memset(ones[:], 1.0), [], ["ones"])
    P.op("pool", lambda g: g.memset(lnh[:], math.log(0.5)), [], ["lnh"])
    P.op("pool", lambda g: g.memset(hist[:], 0.0), [], ["hist"])
    P.op("pool", lambda g: g.memset(hcarry[:], 0.0), [], ["hcarry"])
    P.op("pool", lambda g: g.memset(S_b[:], 0.0), [], ["S_b"])
    ts("pool", hgT[:], hgT[:], 0.5, None, ALU.mult, None, ["hgT"], ["hgT"])
    ts("pool", nbg2[:], nbg2[:], -1.0, None, ALU.mult, None, ["nbg2"], ["nbg2"])
    ts("pool", hbg[:], pvec[:, :, 1:3], 0.5, None, ALU.mult, None, ["pvec"], ["hbg"])
    act(c8[:], pvec[:, :, 3], AF.Exp, ["pvec"], ["c8"], scale=-1.0)
    act(c8[:], c8[:], AF.Ln, ["c8"], ["c8"], bias=1.0)
    ts("dve", hc8[:], c8[:], -4.0, None, ALU.mult, None, ["c8"], ["hc8"])
    ts("dve", c8[:], c8[:], -8.0, None, ALU.mult, None, ["c8"], ["c8"])

    cTs = P.sb(A, "cTs", [128, 8, NTOK], F32)
    tcT = P.sb(A, "tcT", [128, 8, NTOK], F32)
    scT = P.sb(A, "scT", [128, 8, NTOK], BF16)
    bchunk = [P.sb(A, f"bchunk{i}", [NTOK, 512], F32) for i in range(2)]
    ada_tok = P.sb(A, "ada_tok", [NTOK, 3 * D], F32)
    adaT = P.sb(A, "adaT", [128, 16], F32)
    badaT = P.sb(A, "badaT", [128, 16], F32)
    gnT = P.sb(A, "gnT", [128, 8], F32)
    sel = P.sb(A, "sel", [NTOK, 128], F32)
    eye16 = P.sb(A, "eye16", [128, NS, NS], F32)

    P.dma("sp", "a0", cTs[:], cT.rearrange("(k p) n -> p k n", p=128), writes=["cTs"])
    P.dma("sp", "a1", badaT[:], b_adaT[:, :], writes=["badaT"])
    P.dma("sp", "a2", gnT[:], g_normT[:, :], writes=["gnT"])
    P.dma("sp", "a3", sel[:], sel_d[:, :], writes=["sel"])
    P.dma("sp", "a4", eye16[:], eye16_d.rearrange("p (a b) -> p a b", a=NS), writes=["eye16"])
    act(tcT[:], cTs[:], AF.Tanh, ["cTs"], ["tcT"], scale=0.5)
    stt(tcT[:], tcT[:], 1.0, cTs[:], ALU.add, ALU.mult, ["tcT", "cTs"], ["tcT"])
    ts("dve", scT[:], tcT[:], 0.5, None, ALU.mult, None, ["tcT"], ["scT"])

    def wslot(cc):
        if cc < 4:
            return w_out_bf[:, 4 * cc:4 * cc + 4, :].rearrange("p a (b c) -> p (a b) c", b=2), ("wout", cc), f"wout{cc}"
        g = (9, 3)[cc - 4]
        a, b = WIN_GROUPS[g]
        return w_in_bf[:, :, a:b], ("win", g), f"win{g}"

    kT, bT = bank(reserve=True)
    for cc in range(6):
        wb, wk, wsem = wslot(cc)
        bc = bchunk[cc % 2]; bk = f"bchunk{cc % 2}"
        c0 = cc * 512
        P.dma("pool", wsem, wb, w_ada[:, c0:c0 + 512].rearrange("(k p) n -> p k n", p=128), writes=[wk], nbytes=2 << 20)
        P.dma("sp", bk, bc[:], b_ada_tok[:, c0:c0 + 512], writes=[bk])
        kb, bb = bank()
        mmg(bb[0:NTOK, 0:512], [(scT[:, k, :], wb[:, k, :]) for k in range(8)], ["scT", wk], [kb])
        tt("dve", ada_tok[:, c0:c0 + 512], bb[0:NTOK, 0:512], bc[:], ALU.add, [kb, bk], ["ada_tok"])
        if cc < 4:
            for f in range(4):
                ft = cc * 4 + f
                mmg(bT[:, ft:ft + 1], [(wb[:, k, f * 128:(f + 1) * 128], scT[:, k, NS:NS + 1]) for k in range(8)], ["scT", wk], [kT])
    tt("dve", adaT[:], bT[:, 0:16], badaT[:], ALU.add, [kT, "badaT"], ["adaT"])
    reserved.clear()
    cp("dve", shiftp[:], adaT[:, 0:8], ["adaT"], ["shiftp"])
    stt(geffp[:], adaT[:, 8:16], 1.0, gnT[:], ALU.add, ALU.mult, ["adaT", "gnT"], ["geffp"])
    for half in range(2):
        kb, bb = bank()
        mmg(bb[:, :], [(sel[:, :], ada_tok[:, 2 * D + half * 512: 2 * D + (half + 1) * 512])], ["sel", "ada_tok"], [kb])
        cp("dve", ygla[:, half * 512:(half + 1) * 512], bb[:, :], [kb], ["ygla"])

    def load_win(g, reads=()):
        a, b = WIN_GROUPS[g]
        P.dma("pool", f"win{g}", w_in_bf[:, :, a:b], w_in[:, a:b].rearrange("(k p) n -> p k n", p=128), reads=list(reads), writes=[("win", g)], nbytes=(b - a) * 4096)

    def load_wout(g, reads=()):
        P.dma("pool", f"wout{g}", w_out_bf[:, 4 * g:4 * g + 4, :],
              w_out[g * 512:(g + 1) * 512, :].rearrange("(k p) n -> p k n", p=128), reads=list(reads), writes=[("wout", g)], nbytes=2 << 20)

    for g in (4, 5, 6, 7, 8):
        load_win(g)
    late = [("win", 0), ("win", 1), ("win", 2), ("win", 9), ("win", 3), ("wout", 0), ("wout", 1), ("wout", 2), ("wout", 3)]
    WOUT = [("wout", g) for g in range(4)]

    def scale_wout():
        for kt in range(8, 16):
            act(w_out_bf[:, kt, :], w_out_bf[:, kt, :], AF.Copy, [("wout", kt // 4), "hgT"], [("wout", kt // 4)], scale=hgT[:, kt - 8:kt - 7])
        for g in range(2):
            act(w_out_bf[:, 4 * g:4 * g + 4, :], w_out_bf[:, 4 * g:4 * g + 4, :], AF.Copy, [("wout", g)], [("wout", g)], scale=0.5)

    xs_t = P.sb(A, "xs_t", [NS, D], F32)
    gnbc = P.sb(A, "gnbc", [NS, D], F32)
    s_junk = P.sb(A, "s_junk", [NS, D], BF16)
    s_ss = P.sb(A, "s_ss", [NS, 1], F32)
    s_ln = P.sb(A, "s_ln", [NS, 1], F32)
    s_rstd = P.sb(A, "s_rstd", [NS, 1], F32)
    s_xn = P.sb(A, "s_xn", [NS, D], F32)
    s_hn = P.sb(A, "s_hn", [NS, D], BF16)
    hnTs = P.sb(A, "hnTs", [128, 8, NS], BF16)
    zTs = P.sb(A, "zTs", [128, 20, NS], F32)
    glrTs = P.sb(A, "glrTs", [16, NS], BF16)
    k_tok = P.sb(A, "k_tok", [NS, 512], F32)
    v_toks = P.sb(A, "v_toks", [NS, D], BF16)
    Ws = P.sb(A, "Ws", [NS, D], BF16)
    tzs = gnbc
    h0T = P.sb(A, "h0T", [128, 8, NS], F32)
    c0T = P.sb(A, "c0T", [128, 8, 3, NS], F32)
    csT_t = P.sb(A, "csT_t", [128, 8, 3, NS], F32)

    P.dma("sp", "s0", xs_t[:], xs[:, :], writes=["xs_t"])
    P.dma("sp", "s1", gnbc[:], g_norm_bc[:, :], writes=["gnbc"])
    P.dma("sp", "s2", h0T[:], h0T_d[:, :, :], writes=["h0T"])
    P.dma("sp", "s3", c0T[:], c0T_d[:, :, :, :], writes=["c0T"])
    act(s_junk[:], xs_t[:], AF.Square, ["xs_t"], ["s_junk", "s_ss"], accum=s_ss[:])
    rstd_from(s_ss[:], s_ln[:], s_rstd[:], D, "s_ss", "s_ln", "s_rstd")
    act(s_xn[:], xs_t[:], AF.Copy, ["xs_t", "s_rstd"], ["s_xn"], scale=s_rstd[:])
    stt(gnbc[:], ada_tok[0:NS, D:2 * D], 1.0, gnbc[:], ALU.add, ALU.mult, ["ada_tok", "gnbc"], ["gnbc"])
    tt("dve", s_xn[:], s_xn[:], gnbc[:], ALU.mult, ["s_xn", "gnbc"], ["s_xn"])
    tt("dve", s_hn[:], s_xn[:], ada_tok[0:NS, 0:D], ALU.add, ["s_xn", "ada_tok"], ["s_hn"])
    kb, bb = bank()
    bbv = bb[:].bitcast(BF16)
    transposes([(bbv[:, k * NS:(k + 1) * NS], s_hn[:, k * 128:(k + 1) * 128]) for k in range(8)], ident[0:NS, 0:NS], ["s_hn", "ident"], [kb])
    cp("dve", hnTs[:], bbv[:, 0:8 * NS].rearrange("p (k n) -> p k n", k=8), [kb], ["hnTs"])

    kb, bb = bank()
    for i, col in enumerate([QO + j * 128 for j in range(4)]):
        mmg(bb[:, i * NS:(i + 1) * NS], [(w_in_bf[:, k, col:col + 128], hnTs[:, k, :]) for k in range(8)],
            ["hnTs"] + win_keys(col, col + 128), [kb])
    cp("dve", zTs[:, 16:20, :], bb[:, 0:4 * NS].rearrange("p (f n) -> p f n", f=4), [kb], ["zTq"])
    kb, bb = bank()
    mmg(bb[0:16, 0:NS], [(w_in_bf[:, k, GLR:GLR + 16], hnTs[:, k, :]) for k in range(8)], ["hnTs"] + win_keys(GLR, GLR + 16), [kb])
    cp("dve", glrTs[:], bb[0:16, 0:NS], [kb], ["glrTs"])
    kb, bb = bank()
    mmg(bb[0:NS, :], [(hnTs[:, k, :], w_in_bf[:, k, KO:KO + 512]) for k in range(8)], ["hnTs"] + win_keys(KO, KO + 512), [kb])
    cp("dve", k_tok[:], bb[0:NS, :], [kb], ["k_tok"])
    for half in range(2):
        kb, bb = bank()
        c0 = VO + half * 512
        mmg(bb[0:NS, :], [(hnTs[:, k, :], w_in_bf[:, k, c0:c0 + 512]) for k in range(8)], ["hnTs"] + win_keys(c0, c0 + 512), [kb])
        cp("dve", v_toks[:, half * 512:(half + 1) * 512], bb[0:NS, :], [kb], ["v_toks"])

    al_s = P.sb(A, "al_s", [128, 4, NS], F32)
    qmask = P.sb(A, "qmask", [128, 4, NS, NS], BF16)
    Sb16 = [P.sb(A, f"Sb16_{i}", [128, 4, 256], BF16) for i in range(2)]
    qT_s = P.sb(A, "qT_s", [128, 4, NS], F32)
    kmb = [P.sb(A, f"kmb{i}", [NS, 512], BF16) for i in range(2)]
    qkp = P.sb(A, "qkp", [NS, 512], F32)
    qk_s = P.sb(A, "qk_s", [NS, 4], F32)
    Sbuf = [P.sb(A, "Sbuf0", [128, 4, 256], F32), S_f, P.sb(A, "Sbuf2", [128, 4, 256], F32),
            P.sb(A, "Sbuf3", [128, 4, 256], F32)]
    NSB = len(Sbuf)
    kb, bb = bank()
    for h in range(4):
        mmg(bb[:, h * NS:(h + 1) * NS], [(wg2[:, h * 128:(h + 1) * 128], glrTs[:, :])], ["wg2", "glrTs"], [kb])
    tt("dve", al_s[:], bb[:, 0:4 * NS].rearrange("p (h n) -> p h n", h=4), nbg2[:].unsqueeze(2).to_broadcast([128, 4, NS]),
       ALU.subtract, [kb, "nbg2"], ["al_s"])
    act(al_s[:], al_s[:], AF.Exp, ["al_s"], ["al_s"], scale=-1.0)
    act(al_s[:], al_s[:], AF.Ln, ["al_s"], ["al_s"], bias=1.0)
    act(al_s[:], al_s[:], AF.Exp, ["al_s"], ["al_s"], scale=-1.0 / 16)
    stt(qT_s[:], zTs[:, 16:20, :], 128 ** -0.5, al_s[:], ALU.mult, ALU.mult, ["zTq", "al_s"], ["qT_s"])
    for h in range(4):
        tt("dve", qmask[:, h, :, :], qT_s[:, h, :].unsqueeze(2).to_broadcast([128, NS, NS]), eye16[:], ALU.mult,
           ["qT_s", "eye16"], ["qmask"])
    kb, bb = bank()
    mmg(bb[0:NS, :], [(hnTs[:, k, :], w_in_bf[:, k, QO:QO + 512]) for k in range(8)], ["hnTs"] + win_keys(QO, QO + 512), [kb])
    tt("dve", qkp[:], bb[0:NS, :], k_tok[:], ALU.mult, [kb, "k_tok"], ["qkp"])
    P.op("dve", lambda v: v.tensor_reduce(out=qk_s[:], in_=qkp[:].rearrange("p (h k) -> p h k", h=4), axis=mybir.AxisListType.X, op=ALU.add),
         ["qkp"], ["qk_s"], cost=0.7)
    ts("dve", qk_s[:], qk_s[:], 128 ** -0.5, None, ALU.mult, None, ["qk_s"], ["qk_s"])
    ko0, bo0 = bank(reserve=True)
    ko1, bo1 = bank(reserve=True)
    obank = [(ko0, bo0), (ko0, bo0), (ko1, bo1), (ko1, bo1)]
    for b in range(NS):
        sbk = f"Sbuf{b % NSB}"; St = Sbuf[b % NSB]
        P.dma("sp", sbk, St[:], S0_d[b].rearrange("h k v -> k h v"), writes=[sbk], nbytes=512 * 1024)
        km = kmb[b % 2]; kmk = f"kmb{b % 2}"
        ts("dve", km[:], k_tok[:], identf[0:NS, b:b + 1], None, ALU.mult, None, ["k_tok", "identf"], [kmk])
        Sh = Sb16[b % 2]; shk = f"Sb16_{b % 2}"
        cp("act", Sh[:], St[:], [sbk], [shk])
        for h in range(4):
            ko, bo = obank[h]
            P.op("pe", lambda pe, h=h, b=b, bo=bo, Sh=Sh: pe.matmul(bo[0:NS, (h % 2) * 256:(h % 2 + 1) * 256], lhsT=qmask[:, h, b, :], rhs=Sh[:, h, :],
                                                                      start=(b == 0 and h % 2 == 0), stop=(b == NS - 1), skip_group_check=True),
                 ["qmask", shk, ko], [ko], cost=0.12)
        for hp in range(2):
            kb, bb = bank()
            for hh in range(2):
                h = hp * 2 + hh
                mmg(bb[:, hh * 256:(hh + 1) * 256], [(km[:, h * 128:(h + 1) * 128], v_toks[:, h * 256:(h + 1) * 256])],
                    [kmk, "v_toks"], [kb])
            for hh in range(2):
                h = hp * 2 + hh
                stt(St[:, h, :], St[:, h, :], al_s[:, h, b:b + 1], bb[:, hh * 256:(hh + 1) * 256], ALU.mult, ALU.add,
                    [sbk, "al_s", kb], [sbk, ("tick", b)])
        P.dma("act", f"Sst{b % NSB}", ss_o[b].rearrange("h k v -> k h v"), St[:], reads=[sbk], nbytes=512 * 1024)
        if b < len(late):
            kind, g = late[b]
            (load_win if kind == "win" else load_wout)(g, reads=[("tick", b)])
    out_dma_keys += [f"Sbuf{i}" for i in range(NSB)]
    scale_wout()
    for half in range(2):
        kb, bb = bank()
        c0 = ZG + half * 512
        sl = slice(half * 512, (half + 1) * 512)
        mmg(bb[0:NS, :], [(hnTs[:, k, :], w_in_bf[:, k, c0:c0 + 512]) for k in range(8)], ["hnTs"] + win_keys(c0, c0 + 512), [kb])
        act(tzs[:, sl], bb[0:NS, :], AF.Tanh, [kb], ["gnbc"], scale=0.5)
        stt(Ws[:, sl], tzs[:, sl], 1.0, bb[0:NS, :], ALU.add, ALU.mult, ["gnbc", kb], ["Ws"])

    kb, bb = bank()
    for i, col in enumerate([XA + j * 128 for j in range(8)] + [ZA + j * 128 for j in range(8)]):
        mmg(bb[:, i * NS:(i + 1) * NS], [(w_in_bf[:, k, col:col + 128], hnTs[:, k, :]) for k in range(8)],
            ["hnTs"] + win_keys(col, col + 128), [kb])
    cp("dve", zTs[:, 0:16, :], bb[:, 0:16 * NS].rearrange("p (f n) -> p f n", f=16), [kb], ["zTs"])
    def bc3(ap2, n=NS):
        return ap2.unsqueeze(2).to_broadcast([128, 8, n])

    xc_s = P.sb(A, "xc_s", [128, 8, NS], F32)
    tmp_s = P.sb(A, "tmp_s", [128, 8, NS], F32)
    xcb_s = P.sb(A, "xcb_s", [128, 8, NS], BF16)
    tx_s = P.sb(A, "tx_s", [128, 8, NS], F32)
    tg_s = P.sb(A, "tg_s", [128, 8, NS], F32)
    a_s = P.sb(A, "a_s", [128, 8, NS], F32)
    a2_s = P.sb(A, "a2_s", [128, 8, NS], F32)
    h_s = P.sb(A, "h_s", [128, 8, NS], F32)
    tz_s = P.sb(A, "tz_s", [128, 8, NS], F32)
    yTs = P.sb(A, "yTs", [128, 16, NS], BF16)
    xaTs = zTs[:, 0:8, :]
    zaTs = zTs[:, 8:16, :]
    tt("dve", xc_s[:], xaTs, bc3(cw[:, :, 3]), ALU.mult, ["zTs", "cw"], ["xc_s"])
    for jj in range(3):
        tt("dve", tmp_s[:], c0T[:, :, jj, :], bc3(cw[:, :, jj]), ALU.mult, ["c0T", "cw"], ["tmp_s"])
        tt("dve", xc_s[:], xc_s[:], tmp_s[:], ALU.add, ["xc_s", "tmp_s"], ["xc_s"])
    tt("dve", xc_s[:], xc_s[:], bc3(pvec[:, :, 0]), ALU.add, ["xc_s", "pvec"], ["xc_s"])
    cp("dve", xcb_s[:], xc_s[:], ["xc_s"], ["xcb_s"])
    kb, bb = bank()
    for j in range(8):
        mmg(bb[:, j * NS:(j + 1) * NS], [(wgx[:, j, :], xcb_s[:, j, :])], ["wgx", "xcb_s"], [kb])
        mmg(bb[:, 128 + j * NS:128 + (j + 1) * NS], [(wga[:, j, :], xcb_s[:, j, :])], ["wga", "xcb_s"], [kb])
    v3 = lambda ap: ap.rearrange("p (j n) -> p j n", j=8)
    stt(tx_s[:], v3(bb[:, 0:128]), 0.5, bc3(hbg[:, :, 0]), ALU.mult, ALU.add, [kb, "hbg"], ["tx_s"])
    stt(tg_s[:], v3(bb[:, 128:256]), 0.5, bc3(hbg[:, :, 1]), ALU.mult, ALU.add, [kb, "hbg"], ["tg_s"])
    act(tx_s[:], tx_s[:], AF.Tanh, ["tx_s"], ["tx_s"])
    act(tg_s[:], tg_s[:], AF.Tanh, ["tg_s"], ["tg_s"])
    stt(tg_s[:], tg_s[:], 1.0, bc3(hc8[:]), ALU.add, ALU.mult, ["tg_s", "hc8"], ["tg_s"])
    act(a_s[:], tg_s[:], AF.Exp, ["tg_s"], ["a_s"])
    act(a2_s[:], tg_s[:], AF.Exp, ["tg_s"], ["a2_s"], scale=2.0)
    act(tz_s[:], zaTs, AF.Tanh, ["zTs"], ["tz_s"], scale=0.5)
    act(a2_s[:], a2_s[:], AF.Ln, ["a2_s"], ["a2_s"], scale=-1.0, bias=1.0)
    act(a2_s[:], a2_s[:], AF.Exp, ["a2_s"], ["a2_s"], scale=0.5)
    stt(tx_s[:], tx_s[:], 1.0, xc_s[:], ALU.add, ALU.mult, ["tx_s", "xc_s"], ["tx_s"])
    stt(tx_s[:], tx_s[:], 0.5, a2_s[:], ALU.mult, ALU.mult, ["tx_s", "a2_s"], ["tx_s"])
    tt("dve", h_s[:], a_s[:], h0T[:], ALU.mult, ["a_s", "h0T"], ["h_s"])
    tt("dve", h_s[:], h_s[:], tx_s[:], ALU.add, ["h_s", "tx_s"], ["h_s"])
    P.dma("sp", "o_hsT", hsT[:, :, :], h_s[:], reads=["h_s"]); out_dma_keys.append("o_hsT")
    stt(tz_s[:], tz_s[:], 1.0, zaTs, ALU.add, ALU.mult, ["tz_s", "zTs"], ["tz_s"])
    tt("dve", yTs[:, 0:8, :], h_s[:], tz_s[:], ALU.mult, ["h_s", "tz_s"], ["yTs"])
    cp("pool", csT_t[:, :, 0:2, :], c0T[:, :, 1:3, :], ["c0T"], ["csT_t"])
    cp("pool", csT_t[:, :, 2, :], xaTs, ["zTs", "csT_t"], ["csT_t"])
    P.dma("sp", "o_csT", csT[:, :, :, :], csT_t[:], reads=["csT_t"]); out_dma_keys.append("o_csT")


    so_ss = P.sb(A, "so_ss", [NS, 4], F32)
    so_ln = P.sb(A, "so_ln", [NS, 4], F32)
    so_r = P.sb(A, "so_r", [NS, 4], F32)
    ygs = P.sb(A, "ygs", [NS, D], BF16)
    o_sb = s_xn
    for h in range(4):
        ko, bo = obank[h]
        hs = slice(h * 256, (h + 1) * 256)
        stt(o_sb[:, hs], v_toks[:, hs], qk_s[:, h:h + 1], bo[0:NS, (h % 2) * 256:(h % 2 + 1) * 256], ALU.mult, ALU.add,
            ["v_toks", "qk_s", ko], ["s_xn"])
        act(ygs[:, 0:256], o_sb[:, hs], AF.Square, ["s_xn"], ["ygs", "so_ss"], accum=so_ss[:, h:h + 1])
    rstd_from(so_ss[:], so_ln[:], so_r[:], 256, "so_ss", "so_ln", "so_r")
    for h in range(4):
        hs = slice(h * 256, (h + 1) * 256)
        stt(ygs[:, hs], o_sb[:, hs], so_r[:, h:h + 1], Ws[:, hs], ALU.mult, ALU.mult, ["s_xn", "so_r", "Ws"], ["ygs"])
    reserved.clear()
    kb, bb = bank()
    bbv = bb[:].bitcast(BF16)
    transposes([(bbv[:, k * NS:(k + 1) * NS], ygs[:, k * 128:(k + 1) * 128]) for k in range(8)], ident[0:NS, 0:NS], ["ygs", "ident"], [kb])
    cp("dve", yTs[:, 8:16, :], bbv[:, 0:8 * NS].rearrange("p (k n) -> p k n", k=8), [kb], ["yTs"])
    r_s = s_xn
    f_ss = P.sb(A, "f_ss", [NS, 1], F32)
    f_ln = P.sb(A, "f_ln", [NS, 1], F32)
    f_r = P.sb(A, "f_r", [NS, 1], F32)
    for half in range(2):
        kb, bb = bank()
        sl = slice(half * 512, (half + 1) * 512)
        mmg(bb[0:NS, :], [(yTs[:, kt, :], w_out_bf[:, kt, sl]) for kt in range(16)], ["yTs"] + WOUT, [kb])
        tt("dve", r_s[:, sl], bb[0:NS, :], ada_tok[0:NS, 2 * D + half * 512:2 * D + (half + 1) * 512], ALU.mult, [kb, "ada_tok"], ["s_xn"])
    tt("dve", r_s[:], r_s[:], xs_t[:], ALU.add, ["s_xn", "xs_t"], ["s_xn"])
    act(s_junk[:], r_s[:], AF.Square, ["s_xn"], ["s_junk", "f_ss"], accum=f_ss[:])
    rstd_from(f_ss[:], f_ln[:], f_r[:], D, "f_ss", "f_ln", "f_r")
    stt(r_s[:], r_s[:], f_r[:], gfin[0:NS, :], ALU.mult, ALU.mult, ["s_xn", "f_r", "gfin"], ["s_xn"])
    P.dma("sp", "o_ys", ys[:, :], r_s[:], reads=["s_xn"]); out_dma_keys.append("o_ys")

    P.barrier()
    A.close()

    B = ExitStack()
    P.op("pool", lambda g: g.memset(S_f[:], 0.0), [], [("S_f", h) for h in range(4)])
    for kt in range(16):
        tt("pool" if kt % 8 in (2, 5, 7) else "dve", w_out_bf[:, kt, :], w_out_bf[:, kt, :], ygla[:, :], ALU.mult,
           [("wout", kt // 4), "ygla"], [("wout", kt // 4)])
    hnTs2 = [P.sb(B, f"hnT{i}", [128, 8, TB], BF16) for i in range(2)]
    yTs2 = [P.sb(B, f"yT{i}", [128, 16, TB], BF16) for i in range(2)]
    v_tok = P.sb(B, "v_tok", [128, 2, D], BF16)
    x_in = P.sb(B, "x_in", [128, D], F32)
    xn = P.sb(B, "xn0", [128, D], BF16)
    rbuf = P.sb(B, "rbuf0", [128, D], F32)
    fjunk = P.sb(B, "fjunk", [128, D], BF16)
    p_ss = P.sb(B, "p_ss", [128, 2], F32); p_ln = P.sb(B, "p_ln", [128, 2], F32); p_r = P.sb(B, "p_r", [128, 2], F32)
    f_ss2 = P.sb(B, "f_ss2", [128, 2], F32); f_ln2 = P.sb(B, "f_ln2", [128, 2], F32); f_r2 = P.sb(B, "f_r2", [128, 2], F32)
    o_ss = P.sb(B, "o_ss", [128, 4], F32); o_ln = P.sb(B, "o_ln", [128, 4], F32); o_r = P.sb(B, "o_r", [128, 4], F32)
    glrT = P.sb(B, "glrT", [16, TB], BF16)
    tE = P.sb(B, "tE", [128, TB], F32)
    Pb = [P.sb(B, f"Pb{i}", [128, TB], F32) for i in range(2)]
    Pinv = [P.sb(B, f"Pinv{i}", [128, TB], F32) for i in range(2)]
    plast = P.sb(B, "plast", [128, 4, 2], F32)
    q_in = P.sb(B, "q_in", [128, 4, TB], BF16)
    k_in = P.sb(B, "k_in", [128, 4, TB], BF16)
    kdecT = P.sb(B, "kdecT", [128, TB], BF16)
    kdec_tok = P.sb(B, "kdec_tok", [128, 4, 2, 128], BF16)
    Am = P.sb(B, "Am", [128, 4, 128], BF16)
    tzg = P.sb(B, "tzg", [128, 512], F32)
    Wt = P.sb(B, "Wt", [128, D], BF16)
    LT = []
    for t in range(2):
        LT.append(dict(
            xe=P.sb(B, f"xa_ext{t}", [128, TB + 3], F32), xc=P.sb(B, f"xc{t}", [128, TB], F32), xcb=P.sb(B, f"xcb{t}", [128, TB], BF16),
            tx=P.sb(B, f"txb{t}", [128, TB], F32), tg=P.sb(B, f"tgb{t}", [128, TB], F32), gxc=P.sb(B, f"gxc{t}", [128, TB], F32),
            Aa=P.sb(B, f"Aa{t}", [128, TB], F32), A2=P.sb(B, f"A2{t}", [128, TB], F32), sz=P.sb(B, f"szb{t}", [128, TB], BF16),
            hb=P.sb(B, f"hbuf{t}", [128, TB], F32)))

    class Ring:
        def __init__(self, idxs):
            self.idxs = idxs
            self.i = 0

        def get(self):
            b = self.idxs[self.i % len(self.idxs)]
            self.i += 1
            return ("ps", b), banks[b]

    ringG = Ring([0, 1, 2])
    ringL = Ring([3, 4, 5])
    ringB = Ring([6, 7])
    XB = 512 * 1024

    def mm_fm(bb, kb, hn, hks, col, ncols):
        mmg(bb[0:ncols, 0:TB], [(w_in_bf[:, k, col:col + ncols], hn[:, k, :]) for k in range(8)], list(hks) + win_keys(col, col + ncols), [kb])

    def prenorm(blk):
        t0 = blk * TB
        hn = hnTs2[blk % 2]; hk = f"hnT{blk % 2}"
        for i in range(2):
            P.dma("sp", "x_in", x_in[:], xp[t0 + i * 128:t0 + (i + 1) * 128, :], writes=["x_in"], nbytes=XB)
            act(xn[:], x_in[:], AF.Square, ["x_in"], ["xn0", ("p_ss", i)], accum=p_ss[:, i:i + 1])
            act(p_ln[:, i:i + 1], p_ss[:, i:i + 1], AF.Ln, [("p_ss", i)], [("p_ln", i)], scale=1.0 / D, bias=EPS)
            act(p_r[:, i:i + 1], p_ln[:, i:i + 1], AF.Exp, [("p_ln", i)], [("p_r", i)], scale=-0.5)
            act(xn[:], x_in[:], AF.Copy, ["x_in", ("p_r", i)], ["xn0"], scale=p_r[:, i:i + 1])
            kb, bb = ringB.get()
            bbv = bb[:].bitcast(BF16)
            transposes([(bbv[:, k * 128:(k + 1) * 128], xn[:, k * 128:(k + 1) * 128]) for k in range(8)], ident[:], ["xn0", "ident"], [kb])
            for k in range(8):
                ts("dve", hn[:, k, i * 128:(i + 1) * 128], bbv[:, k * 128:(k + 1) * 128], geffp[:, k:k + 1], shiftp[:, k:k + 1],
                   ALU.mult, ALU.add, [kb, "geffp", "shiftp"], [(hk, i)])

    def gla(blk):
        hn = hnTs2[blk % 2]; hk = f"hnT{blk % 2}"
        HK = [(hk, 0), (hk, 1)]
        yT = yTs2[blk % 2]; yk = f"yT{blk % 2}"
        for i in range(2):
            for half in range(2):
                kb, bb = ringG.get()
                c0 = VO + half * 512
                mmg(bb[:, :], [(hn[:, k, i * 128:(i + 1) * 128], w_in_bf[:, k, c0:c0 + 512]) for k in range(8)], [(hk, i)] + win_keys(c0, c0 + 512), [kb])
                cp("act", v_tok[:, i, half * 512:(half + 1) * 512], bb[:, :], [kb], [("v_tok", i)])
        kb, bb = ringG.get()
        mmg(bb[0:16, 0:TB], [(w_in_bf[:, k, GLR:GLR + 16], hn[:, k, :]) for k in range(8)], HK + win_keys(GLR, GLR + 16), [kb])
        cp("act", glrT[:], bb[0:16, 0:TB], [kb], ["glrT"])
        for h in range(4):
            Pp = Pb[h % 2]; pk = f"Pb{h % 2}"; Pi = Pinv[h % 2]; pik = f"Pinv{h % 2}"
            kb, bb = ringG.get()
            mmg(bb[:, 0:TB], [(wg2[:, h * 128:(h + 1) * 128], glrT[:, :])], ["wg2", "glrT"], [kb])
            act(tE[:], bb[:, 0:TB], AF.Exp, [kb, "nbg2"], ["tE"], scale=-1.0, bias=nbg2[:, h:h + 1])
            act(tE[:], tE[:], AF.Ln, ["tE"], ["tE"], bias=1.0)
            for c in range(2):
                cs = slice(c * 128, (c + 1) * 128)
                scan(Pp[:, cs], ones[:, :], tE[:, cs], 0.0, ["tE", "ones"], [(pk, c)])
            act(Pi[:], Pp[:], AF.Exp, [(pk, 0), (pk, 1)], [pik], scale=1.0 / 16)
            act(Pp[:], Pp[:], AF.Exp, [(pk, 0), (pk, 1)], [(pk, 0), (pk, 1)], scale=-1.0 / 16)
            cp("pool", plast[:, h, :], Pp[:].rearrange("p (c t) -> p c t", c=2)[:, :, 127], [(pk, 0), (pk, 1)], [("plast", h)])
            kb, bb = ringG.get()
            mm_fm(bb, kb, hn, HK, QO + h * 128, 128)
            stt(q_in[:, h, :], bb[:, 0:TB], 128 ** -0.5, Pp[:], ALU.mult, ALU.mult, [kb, (pk, 0), (pk, 1)], [("q_in", h)])
            kb, bb = ringG.get()
            mm_fm(bb, kb, hn, HK, KO + h * 128, 128)
            tt("dve", k_in[:, h, :], bb[:, 0:TB], Pi[:], ALU.mult, [kb, pik], [("k_in", h)])
            for c in range(2):
                cs = slice(c * 128, (c + 1) * 128)
                stt(kdecT[:, cs], bb[:, cs], plast[:, h, c:c + 1], Pi[:, cs], ALU.mult, ALU.mult, [kb, ("plast", h), pik], [("kdecT", c)])
            kb2, bb2 = ringG.get()
            bbv = bb2[:].bitcast(BF16)
            transposes([(bbv[:, c * 128:(c + 1) * 128], kdecT[:, c * 128:(c + 1) * 128]) for c in range(2)], ident[:],
                       [("kdecT", 0), ("kdecT", 1), "ident"], [kb2])
            cp("act", kdec_tok[:, h, :, :], bbv[:, 0:256].rearrange("p (c k) -> p c k", c=2), [kb2], [("kdec_tok", h)])
        QK = [("q_in", h) for h in range(4)] + [("k_in", h) for h in range(4)]
        for c in range(2):
            cs = slice(c * 128, (c + 1) * 128)
            for half in range(2):
                kb, bb = ringG.get()
                c0 = ZG + half * 512
                mmg(bb[:, :], [(hn[:, k, cs], w_in_bf[:, k, c0:c0 + 512]) for k in range(8)], [(hk, c)] + win_keys(c0, c0 + 512), [kb])
                act(tzg[:], bb[:, :], AF.Tanh, [kb], ["tzg"], scale=0.5)
                stt(Wt[:, half * 512:(half + 1) * 512], tzg[:], 1.0, bb[:, :], ALU.add, ALU.mult, ["tzg", kb], [("Wt", half)])
            kA, bA = ringG.get()
            for h in range(4):
                mmg(bA[:, h * 128:(h + 1) * 128], [(k_in[:, h, cs], q_in[:, h, cs])], QK, [kA])
            tt("dve", Am[:], bA[:, :].rearrange("p (h t) -> p h t", h=4), maskT[:].unsqueeze(1).to_broadcast([128, 4, 128]), ALU.mult,
               [kA, "maskT"], ["Am"])
            ksu = [ringG.get(), ringG.get()]
            for h in range(4):
                kk, bs = ksu[h // 2]
                mmg(bs[:, (h % 2) * 256:(h % 2 + 1) * 256], [(kdec_tok[:, h, c, :], v_tok[:, c, h * 256:(h + 1) * 256])], [("kdec_tok", h), ("v_tok", c)], [kk])
            for h in range(4):
                kk, bs = ksu[h // 2]
                stt(S_f[:, h, :], S_f[:, h, :], plast[:, h, c:c + 1], bs[:, (h % 2) * 256:(h % 2 + 1) * 256], ALU.mult, ALU.add,
                    [("S_f", h), ("plast", h), kk], [("S_f", h)])
            ko = [ringG.get(), ringG.get()]
            for h in range(4):
                kk, bo = ko[h // 2]
                mmg(bo[:, (h % 2) * 256:(h % 2 + 1) * 256],
                    [(Am[:, h, :], v_tok[:, c, h * 256:(h + 1) * 256]), (q_in[:, h, cs], S_b[:, h, :])], ["Am", ("v_tok", c), ("q_in", h), ("S_b", h)], [kk])
            for h in range(4):
                cp("pool", S_b[:, h, :], S_f[:, h, :], [("S_f", h)], [("S_b", h)])
            junk_o = fjunk[:, 0:256]
            for h in range(4):
                kk, bo = ko[h // 2]
                act(junk_o, bo[:, (h % 2) * 256:(h % 2 + 1) * 256], AF.Square, [kk], ["fjunk", "o_ss"], accum=o_ss[:, h:h + 1])
            rstd_from(o_ss[:], o_ln[:], o_r[:], 256, "o_ss", "o_ln", "o_r")
            for h in range(4):
                kk, bo = ko[h // 2]
                stt(ygla[:, h * 256:(h + 1) * 256], bo[:, (h % 2) * 256:(h % 2 + 1) * 256], o_r[:, h:h + 1], Wt[:, h * 256:(h + 1) * 256],
                    ALU.mult, ALU.mult, [kk, "o_r", ("Wt", h // 2)], ["ygla"])
            kb, bb = ringG.get()
            bbv = bb[:].bitcast(BF16)
            transposes([(bbv[:, k * 128:(k + 1) * 128], ygla[:, k * 128:(k + 1) * 128]) for k in range(8)], ident[:], ["ygla", "ident"], [kb])
            cp("act", yT[:, 8:16, cs], bbv[:, :].rearrange("p (k t) -> p k t", k=8), [kb], [(yk, "g", c)])

    def lru(blk, j):
        t = j % 2
        hn = hnTs2[blk % 2]; hk = f"hnT{blk % 2}"
        HK = [(hk, 0), (hk, 1)]
        yT = yTs2[blk % 2]; yk = f"yT{blk % 2}"
        L = LT[t]
        xe, xc, xcb, tx, tg, gxc, Aa, A2, sz, hb = (L[k] for k in ("xe", "xc", "xcb", "tx", "tg", "gxc", "Aa", "A2", "sz", "hb"))
        K = lambda n: f"{n}{t}"
        kb, bb = ringL.get()
        mm_fm(bb, kb, hn, HK, XA + j * 128, 128)
        cp("pool", xe[:, 0:3], hist[:, j, :], [("hist", j)], [K("xeh")])
        cp("act", xe[:, 3:TB + 3], bb[:, 0:TB], [kb], [K("xe")])
        cp("pool", hist[:, j, :], xe[:, TB:TB + 3], [K("xe")], [("hist", j)])
        ts("dve", xc[:], xe[:, 0:TB], cw[:, j, 0:1], pvec[:, j, 0:1], ALU.mult, ALU.add, [K("xe"), K("xeh"), "cw", "pvec"], [K("xc")])
        for jj in range(1, 4):
            stt(xc[:], xe[:, jj:jj + TB], cw[:, j, jj:jj + 1], xc[:], ALU.mult, ALU.add, [K("xe"), K("xeh"), "cw", K("xc")], [K("xc")])
        cp("pool", xcb[:], xc[:], [K("xc")], [K("xcb")])
        kg, bg = ringL.get()
        mmg(bg[:, 0:TB], [(wgx[:, j, :], xcb[:])], ["wgx", K("xcb")], [kg])
        mmg(bg[:, TB:2 * TB], [(wga[:, j, :], xcb[:])], ["wga", K("xcb")], [kg])
        act(tx[:], bg[:, 0:TB], AF.Tanh, [kg, "hbg"], [K("tx")], scale=0.5, bias=hbg[:, j, 0:1])
        act(tg[:], bg[:, TB:2 * TB], AF.Tanh, [kg, "hbg"], [K("tg")], scale=0.5, bias=hbg[:, j, 1:2])
        act(Aa[:], tg[:], AF.Exp, [K("tg"), "hc8"], [K("Aa")], scale=hc8[:, j:j + 1], bias=hc8[:, j:j + 1])
        act(A2[:], tg[:], AF.Exp, [K("tg"), "c8"], [K("A2")], scale=c8[:, j:j + 1], bias=c8[:, j:j + 1])
        stt(gxc[:], tx[:], 1.0, xc[:], ALU.add, ALU.mult, [K("tx"), K("xc")], [K("gxc")])
        kz, bz = ringL.get()
        mm_fm(bz, kz, hn, HK, ZA + j * 128, 128)
        act(tx[:], bz[:, 0:TB], AF.Tanh, [kz], [K("tx")], scale=0.5)
        stt(sz[:], tx[:], 1.0, bz[:, 0:TB], ALU.add, ALU.mult, [K("tx"), kz], [K("sz")])
        act(A2[:], A2[:], AF.Ln, [K("A2")], [K("A2")], scale=-1.0, bias=1.0)
        act(A2[:], A2[:], AF.Exp, [K("A2")], [K("A2")], scale=0.5, bias=lnh[:, 0:1])
        tt("pool", gxc[:], gxc[:], A2[:], ALU.mult, [K("gxc"), K("A2")], [K("gxc")])
        scan(hb[:], Aa[:], gxc[:], hcarry[:, j:j + 1], [K("Aa"), K("gxc"), ("hcarry", j)], [K("hb")])
        cp("pool", hcarry[:, j:j + 1], hb[:, TB - 1:TB], [K("hb")], [("hcarry", j)])
        tt("pool", yT[:, j, :], hb[:], sz[:], ALU.mult, [K("hb"), K("sz")], [(yk, j)])

    def outproj(blk):
        t0 = blk * TB
        yT = yTs2[blk % 2]; yk = f"yT{blk % 2}"
        YT_ALL = [(yk, j) for j in range(8)] + [(yk, "g", c) for c in range(2)]
        rb = rbuf; rk = "rbuf0"
        for i in range(2):
            P.dma("sp", rk, rb[:], xp[t0 + i * 128:t0 + (i + 1) * 128, :], writes=[rk], nbytes=XB)
            for half in range(2):
                kb, bb = ringB.get()
                sl = slice(half * 512, (half + 1) * 512)
                mmg(bb[:, :], [(yT[:, kt, i * 128:(i + 1) * 128], w_out_bf[:, kt, sl]) for kt in range(16)], YT_ALL + WOUT, [kb])
                tt("dve", rb[:, sl], bb[:, :], rb[:, sl], ALU.add, [kb, rk], [rk])
            act(fjunk[:], rb[:], AF.Square, [rk], ["fjunk", ("f_ss", i)], accum=f_ss2[:, i:i + 1])
            act(f_ln2[:, i:i + 1], f_ss2[:, i:i + 1], AF.Ln, [("f_ss", i)], [("f_ln", i)], scale=1.0 / D, bias=EPS)
            act(f_r2[:, i:i + 1], f_ln2[:, i:i + 1], AF.Exp, [("f_ln", i)], [("f_r", i)], scale=-0.5)
            stt(rb[:], rb[:], f_r2[:, i:i + 1], gfin[:], ALU.mult, ALU.mult, [rk, ("f_r", i), "gfin"], [rk])
            P.dma("sp", rk, yp[t0 + i * 128:t0 + (i + 1) * 128, :], rb[:], reads=[rk], nbytes=XB)

    prenorm(0)
    for blk in range(NB):
        gla(blk)
        for j in range(8):
            lru(blk, j)
        if blk + 1 < NB:
            prenorm(blk + 1)
        outproj(blk)

    P.dma("sp", "o_hpT", hpT[:, :], hcarry[:], reads=[("hcarry", j) for j in range(8)]); out_dma_keys.append("o_hpT")
    P.dma("sp", "o_cpT", cpT[:, :, :], hist[:], reads=[("hist", j) for j in range(8)]); out_dma_keys.append("o_cpT")
    for h in range(4):
        P.dma("sp", "o_sp", sp_o[h], S_f[:, h, :], reads=[("S_f", h)])
    out_dma_keys.append("o_sp")
    P.barrier()
    P.flush()
    B.close()
    P.keep.close()
    build_nc.sim_log = P.sim_log
    return nc


def _prep_inputs(inp):
    f = lambda a: np.ascontiguousarray(a, dtype=np.float32)
    fm8 = lambda v: f(np.asarray(v).reshape(8, 128).T)
    shared = {
        "w_ada": f(inp["w_ada"][0]),
        "b_ada_tok": f(np.tile(np.asarray(inp["b_ada"][0])[None, :], (NTOK, 1))),
        "b_adaT": f(np.asarray(inp["b_ada"][0])[:2 * D].reshape(16, 128).T),
        "g_normT": fm8(inp["g_norm"][0]),
        "g_norm_bc": f(np.tile(np.asarray(inp["g_norm"][0])[None, :], (NS, 1))),
        "w_in": f(inp["w_in"][0]),
        "cw": f(np.asarray(inp["conv_w"][0]).reshape(4, 8, 128).transpose(2, 1, 0)),
        "pvec": f(np.stack([fm8(inp["conv_b"][0]), fm8(inp["b_gate_x"][0]), fm8(inp["b_gate_a"][0]), fm8(inp["lru_lambda"][0])], axis=2)),
        "wgx": f(np.asarray(inp["w_gate_x"][0]).transpose(1, 0, 2)),
        "wga": f(np.asarray(inp["w_gate_a"][0]).transpose(1, 0, 2)),
        "w_g2": f(inp["w_gla_g2"][0]),
        "b_g2T": f(np.asarray(inp["b_gla_g2"][0]).reshape(4, 128).T),
        "g_glaT": fm8(np.asarray(inp["g_gla_norm"][0]).reshape(D)),
        "w_out": f(inp["w_out"][0]),
        "g_final_bc": f(np.tile(np.asarray(inp["g_final"]).reshape(1, D), (128, 1))),
        "ident": np.eye(128, dtype=np.float32),
        "maskT": np.triu(np.ones((128, 128), dtype=np.float32)),
        "sel": f(np.concatenate([np.zeros((NS, 128)), np.ones((1, 128))], axis=0)),
        "eye16": f(np.tile(np.eye(NS, dtype=np.float32).reshape(1, NS * NS), (128, 1))),
    }
    maps = []
    for c in range(8):
        rows = slice(NS * c, NS * (c + 1))
        m = dict(shared)
        m["xp"] = f(inp["x_prompt"][c])
        m["xs"] = f(np.asarray(inp["x_sample"])[rows, 0, :])
        m["cT"] = f(np.concatenate([np.asarray(inp["c_sample"])[rows], np.asarray(inp["c_prompt"])[c:c + 1]], axis=0).T)
        m["h0T"] = f(np.asarray(inp["state_lru_h"])[0, rows].T.reshape(8, 128, NS).transpose(1, 0, 2))
        m["c0T"] = f(np.asarray(inp["state_lru_conv"])[0, rows].transpose(2, 1, 0).reshape(8, 128, 3, NS).transpose(1, 0, 2, 3))
        m["S0"] = f(np.asarray(inp["state_gla"])[0, rows])
        maps.append(m)
    return maps


def kernel(**inputs):
    maps = _prep_inputs(inputs)
    nc = build_nc()
    res = run_bass_kernel_spmd(nc, maps, core_ids=list(range(8)))
    R = res.results
    y_prompt = np.stack([R[c]["yp"] for c in range(8)], axis=0).astype(np.float32)
    y_sample = np.concatenate([R[c]["ys"] for c in range(8)], axis=0)[:, None, :].astype(np.float32)
    hp = np.stack([R[c]["hpT"].T.reshape(D) for c in range(8)], axis=0)[None].astype(np.float32)
    cpo = np.stack([R[c]["cpT"].transpose(2, 1, 0).reshape(3, D) for c in range(8)], axis=0)[None].astype(np.float32)
    spo = np.stack([R[c]["sp"] for c in range(8)], axis=0)[None].astype(np.float32)
    hs = np.concatenate([R[c]["hsT"].transpose(2, 1, 0).reshape(NS, D) for c in range(8)], axis=0)[None].astype(np.float32)
    cso = np.concatenate([R[c]["csT"].transpose(3, 2, 1, 0).reshape(NS, 3, D) for c in range(8)], axis=0)[None].astype(np.float32)
    sso = np.concatenate([R[c]["ss"] for c in range(8)], axis=0)[None].astype(np.float32)
    return (y_prompt, y_sample, hp, cpo, spo, hs, cso, sso)
```

```python
import math
from contextlib import ExitStack

import numpy as np
import concourse.bass as bass
import concourse.mybir as mybir
from concourse.bass_utils import run_bass_kernel_spmd

F32 = mybir.dt.float32
F32R = mybir.dt.float32r
BF16 = mybir.dt.bfloat16
AF = mybir.ActivationFunctionType
ALU = mybir.AluOpType

COMPUTE = ("pe", "act", "dve", "pool")
ENGS = {"pe": "tensor", "act": "scalar", "dve": "vector", "pool": "gpsimd", "sp": "sync"}

D = 1024
T = 2048
TB = 256
NB = T // TB
DIN = 5136
XA, ZA, QO, KO, VO, GLR, ZG = 0, 1024, 2048, 2560, 3072, 4096, 4112
NS = 16
NTOK = 17
EPS = 1e-6
WIN_GROUPS = [(i * 512, (i + 1) * 512) for i in range(8)] + [(4096, 4624), (4624, 5136)]


class Op:
    __slots__ = ("idx", "eng", "fn", "kind", "cost", "lat", "tset", "semkey", "preds", "succs", "nun", "ready", "count", "start", "finish", "dma", "nbytes")


class Prog:
    def __init__(self, nc):
        self.nc = nc
        self.ops = []
        self.res = {}
        self.cnt = {e: 0 for e in COMPUTE}
        self.dcnt = {}
        self.waited = {e: {} for e in ENGS}
        self.keep = ExitStack()
        self.sems = {}
        self.bank_i = 0
        self.barrier_vals = None
        self.act_set = "L"
        self.sim_log = []

    def sb(self, stack, name, shape, dtype):
        return stack.enter_context(self.nc.sbuf_tensor("sb_" + name, list(shape), dtype))

    def _add(self, o, reads, writes):
        o.idx = len(self.ops)
        preds = set()
        for r in reads:
            st = self.res.get(r)
            if st and st[0] is not None:
                preds.add(st[0])
        for w in writes:
            st = self.res.get(w)
            if st:
                if st[0] is not None:
                    preds.add(st[0])
                preds.update(st[1])
        preds.discard(o)
        o.preds = sorted(preds, key=lambda p: p.idx)
        o.succs = []
        for p in o.preds:
            p.succs.append(o)
        for r in reads:
            st = self.res.setdefault(r, [None, []])
            st[1].append(o)
        for w in writes:
            self.res[w] = [o, []]
        self.ops.append(o)

    def op(self, eng, fn, reads=(), writes=(), cost=0.3, tset=None):
        o = Op()
        o.eng = eng; o.fn = fn; o.kind = "op"; o.cost = cost; o.lat = 0.0; o.tset = tset; o.semkey = eng; o.dma = None
        self._add(o, reads, writes)

    def dma(self, q, semkey, out, in_, reads=(), writes=(), nbytes=0):
        o = Op()
        o.eng = q; o.fn = None; o.kind = "dma"; o.cost = 0.07; o.lat = 2.0; o.tset = None
        o.nbytes = nbytes
        o.semkey = ("dma", semkey); o.dma = (out, in_)
        self._add(o, reads, writes)

    def _sem(self, key):
        if key not in self.sems:
            nm = ("s_" + key) if isinstance(key, str) else ("d_" + key[1])
            self.sems[key] = self.keep.enter_context(self.nc.semaphore(nm))
        return self.sems[key]

    def barrier(self):
        self.flush()
        self.barrier_vals = (dict(self.cnt), dict(self.dcnt))
        self.res.clear()

    def _schedule(self):
        ops = self.ops
        import os
        if os.environ.get("SCHED", "1") == "0":
            order = {e: [] for e in ENGS}
            for o in ops:
                order[o.eng].append(o)
            self.sim_time = 0.0
            return order
        ready = {e: [] for e in ENGS}
        free = {e: 0.0 for e in ENGS}
        order = {e: [] for e in ENGS}
        for o in ops:
            o.nun = len(o.preds)
            o.ready = 0.0
            if o.nun == 0:
                ready[o.eng].append(o)
        cur = self.act_set
        left = len(ops)
        dma_free = {e: 0.0 for e in ENGS}
        import os
        BLL = float(os.environ.get("SCH_BLL", "0.2"))
        bl = {}
        for o in reversed(ops):
            m = 0.0
            for s_ in o.succs:
                v = bl[s_] + (BLL if s_.eng != o.eng else 0.0)
                if v > m:
                    m = v
            bl[o] = m + o.cost + o.lat
        WIN = float(os.environ.get("SCH_WIN", "0.3"))
        LAT = float(os.environ.get("SCH_LAT", "0.45"))
        SEED = int(os.environ.get("SCH_SEED", "0"))
        if SEED:
            rng = np.random.default_rng(SEED)
            jit = rng.uniform(0.95, 1.05, size=len(ops))
            for o, jv in zip(ops, jit):
                o.cost *= float(jv)
        PEN = float(os.environ.get("SCH_PEN", "1.3"))
        OVH = float(os.environ.get("SCH_OVH", "0.0"))
        while left:
            best = None
            for e in ENGS:
                fe = free[e]
                cands = []
                mst = None
                for o in ready[e]:
                    st = o.ready if o.ready > fe else fe
                    if e == "act" and o.tset is not None and o.tset != cur:
                        st += PEN
                    cands.append((st, o))
                    if mst is None or st < mst:
                        mst = st
                if not cands:
                    continue
                sel = None
                for st, o in cands:
                    if st <= mst + WIN:
                        k2 = (-bl[o], o.idx)
                        if sel is None or k2 < sel[0]:
                            sel = (k2, st, o)
                key = (mst, sel[2].idx)
                if best is None or key < best[0]:
                    best = (key, sel[2], sel[1])
            _, o, st = best
            e = o.eng
            ready[e].remove(o)
            if e == "act" and o.tset is not None:
                cur = o.tset
            o.start = st
            free[e] = st + o.cost + OVH
            if o.kind == "dma":
                x0 = max(st + o.cost, dma_free[e])
                dma_free[e] = x0 + o.nbytes / (330e3 if e == "pool" else 250e3)
                o.finish = dma_free[e] + o.lat
            else:
                o.finish = st + o.cost
            order[e].append(o)
            left -= 1
            for s_ in o.succs:
                fin = o.finish if s_.eng == e else o.finish + LAT
                if fin > s_.ready:
                    s_.ready = fin
                s_.nun -= 1
                if s_.nun == 0:
                    ready[s_.eng].append(s_)
        self.act_set = cur
        self.sim_time = max(free.values())
        busy = {e: 0.0 for e in ENGS}
        for o in ops:
            busy[o.eng] += o.cost
        self.sim_log.append((len(ops), round(self.sim_time, 1), {e: round(v, 1) for e, v in busy.items()}))
        return order

    def flush(self):
        nc = self.nc
        if not self.ops and self.barrier_vals is None:
            return
        order = self._schedule()
        streams = {e: [] for e in ENGS}
        if self.barrier_vals is not None:
            cv, dv = self.barrier_vals
            for e in ENGS:
                for k, v in cv.items():
                    if k != e and v > self.waited[e].get(k, 0):
                        self.waited[e][k] = v
                        streams[e].append(("wait", k, v))
                for k, v in dv.items():
                    kk = ("dma", k)
                    if v > self.waited[e].get(kk, 0):
                        self.waited[e][kk] = v
                        streams[e].append(("wait", kk, v))
            self.barrier_vals = None
        for e in ENGS:
            for o in order[e]:
                if o.kind == "op":
                    self.cnt[e] += 1
                    o.count = self.cnt[e]
                else:
                    k = o.semkey[1]
                    self.dcnt[k] = self.dcnt.get(k, 0) + 16
                    o.count = self.dcnt[k]
        for e in ENGS:
            w = self.waited[e]
            for o in order[e]:
                need = {}
                for p in o.preds:
                    if p.kind == "op" and p.eng == "pe" and e == "pe":
                        continue
                    if w.get(p.semkey, 0) >= p.count:
                        continue
                    if p.count > need.get(p.semkey, 0):
                        need[p.semkey] = p.count
                for k, v in need.items():
                    w[k] = v
                    streams[e].append(("wait", k, v))
                streams[e].append(("x", o))
        for st in streams.values():
            for it in st:
                if it[0] == "wait":
                    self._sem(it[1])
                elif it[1].kind == "dma":
                    self._sem(it[1].semkey)
        for e in COMPUTE:
            self._sem(e)

        def replay(name, eng):
            for it in streams[name]:
                if it[0] == "wait":
                    eng.wait_ge(self._sem(it[1]), it[2])
                else:
                    o = it[1]
                    if o.kind == "op":
                        o.fn(eng).then_inc(self._sem(name), 1)
                    else:
                        eng.dma_start(out=o.dma[0], in_=o.dma[1]).then_inc(self._sem(o.semkey), 16)

        with nc.Block() as block:
            for name, attr in ENGS.items():
                if not streams[name]:
                    continue
                getattr(block, attr)(lambda eng, name=name: replay(name, eng))
        self.ops = []


def win_keys(c0, c1):
    return [("win", g) for g, (a, b) in enumerate(WIN_GROUPS) if a < c1 and c0 < b]


def build_nc():
    nc = bass.Bass("TRN2", target_bir_lowering=False)
    P = Prog(nc)
    keep = P.keep

    def din(name, shape):
        return nc.dram_tensor(name, list(shape), F32, kind="ExternalInput").ap()

    def dout(name, shape):
        return nc.dram_tensor(name, list(shape), F32, kind="ExternalOutput").ap()

    xp = din("xp", [T, D]); xs = din("xs", [NS, D]); cT = din("cT", [D, NTOK])
    w_ada = din("w_ada", [D, 3 * D]); b_ada_tok = din("b_ada_tok", [NTOK, 3 * D]); b_adaT = din("b_adaT", [128, 16])
    g_normT = din("g_normT", [128, 8]); g_norm_bc = din("g_norm_bc", [NS, D])
    w_in = din("w_in", [D, DIN]); cw_d = din("cw", [128, 8, 4]); pvec_d = din("pvec", [128, 8, 4])
    wgx_d = din("wgx", [128, 8, 128]); wga_d = din("wga", [128, 8, 128])
    w_g2 = din("w_g2", [16, 512]); b_g2T = din("b_g2T", [128, 4]); g_glaT = din("g_glaT", [128, 8])
    w_out = din("w_out", [2 * D, D]); g_final_bc = din("g_final_bc", [128, D])
    h0T_d = din("h0T", [128, 8, NS]); c0T_d = din("c0T", [128, 8, 3, NS]); S0_d = din("S0", [NS, 4, 128, 256])
    ident_d = din("ident", [128, 128]); maskT_d = din("maskT", [128, 128]); sel_d = din("sel", [NTOK, 128])
    eye16_d = din("eye16", [128, NS * NS])

    yp = dout("yp", [T, D]); ys = dout("ys", [NS, D]); hpT = dout("hpT", [128, 8]); cpT = dout("cpT", [128, 8, 3])
    sp_o = dout("sp", [4, 128, 256]); hsT = dout("hsT", [128, 8, NS]); csT = dout("csT", [128, 8, 3, NS])
    ss_o = dout("ss", [NS, 4, 128, 256])
    out_dma_keys = []

    w_in_bf = P.sb(keep, "w_in_bf", [128, 8, DIN], BF16)
    w_out_bf = P.sb(keep, "w_out_bf", [128, 16, D], BF16)
    wgx = P.sb(keep, "wgx_bf", [128, 8, 128], BF16)
    wga = P.sb(keep, "wga_bf", [128, 8, 128], BF16)
    wg2 = P.sb(keep, "wg2_bf", [16, 512], BF16)
    hgT = P.sb(keep, "hgT", [128, 8], F32)
    gfin = P.sb(keep, "gfin", [128, D], F32)
    ident = P.sb(keep, "ident_bf", [128, 128], BF16)
    maskT = P.sb(keep, "maskT", [128, 128], F32)
    ones = P.sb(keep, "ones", [128, 128], F32)
    lnh = P.sb(keep, "lnh", [128, 1], F32)
    cw = P.sb(keep, "cw", [128, 8, 4], F32)
    pvec = P.sb(keep, "pvec", [128, 8, 4], F32)
    hbg = P.sb(keep, "hbg", [128, 8, 2], F32)
    c8 = P.sb(keep, "c8", [128, 8], F32)
    hc8 = P.sb(keep, "hc8", [128, 8], F32)
    geffp = P.sb(keep, "geffp", [128, 8], F32)
    shiftp = P.sb(keep, "shiftp", [128, 8], F32)
    nbg2 = P.sb(keep, "nbg2", [128, 4], F32)
    hist = P.sb(keep, "hist", [128, 8, 3], F32)
    hcarry = P.sb(keep, "hcarry", [128, 8], F32)
    S_f = P.sb(keep, "S_f", [128, 4, 256], F32)
    S_b = P.sb(keep, "S_b", [128, 4, 256], BF16)
    ygla = P.sb(keep, "ygla", [128, D], BF16)
    banks = [keep.enter_context(nc.psum_tensor(f"bank{i}", [128, 512], F32)) for i in range(8)]

    reserved = set()

    def bank(reserve=False):
        while P.bank_i % 8 in reserved:
            P.bank_i += 1
        i = P.bank_i % 8
        P.bank_i += 1
        if reserve:
            reserved.add(i)
        return ("ps", i), banks[i]

    def nfree(ap):
        n = 1
        for d in ap.shape[1:]:
            n *= d
        return n

    def act(out, in_, func, reads, writes, scale=1.0, bias=None, accum=None):
        kw = {}
        if bias is not None:
            kw["bias"] = bias
        if accum is not None:
            kw["accum_out"] = accum
        tset = "T" if func == AF.Tanh else ("L" if func == AF.Ln else None)
        P.op("act", lambda a: a.activation(out=out, in_=in_, func=func, scale=scale, **kw), reads, writes,
             cost=0.2 + nfree(out) / 1200.0 + (0.1 if accum is not None else 0.0), tset=tset)

    def ecost(eng, n, k=1.0):
        if eng == "pool":
            return 0.25 + n / 330.0
        return 0.15 + k * n / 960.0

    def tt(eng, out, in0, in1, op, reads, writes):
        P.op(eng, lambda v: v.tensor_tensor(out=out, in0=in0, in1=in1, op=op), reads, writes, cost=ecost(eng, nfree(out)))

    def ts(eng, out, in0, s1, s2, op0, op1, reads, writes):
        c = ecost(eng, nfree(out), 0.7)
        if s2 is None:
            P.op(eng, lambda v: v.tensor_scalar(out=out, in0=in0, scalar1=s1, scalar2=None, op0=op0), reads, writes, cost=c)
        else:
            P.op(eng, lambda v: v.tensor_scalar(out=out, in0=in0, scalar1=s1, scalar2=s2, op0=op0, op1=op1), reads, writes, cost=c)

    def stt(out, in0, scalar, in1, op0, op1, reads, writes):
        P.op("dve", lambda v: v.scalar_tensor_tensor(out=out, in0=in0, scalar=scalar, in1=in1, op0=op0, op1=op1), reads, writes,
             cost=ecost("dve", nfree(out)))

    def scan(out, d0, d1, initial, reads, writes):
        P.op("dve", lambda v: v.tensor_tensor_scan(out=out, data0=d0, data1=d1, initial=initial, op0=ALU.mult, op1=ALU.add), reads, writes,
             cost=0.15 + 2.0 * nfree(out) / 960.0)

    def cp(eng, out, in_, reads, writes):
        if eng == "act":
            act(out, in_, AF.Copy, reads, writes)
        else:
            P.op(eng, lambda v: v.tensor_copy(out=out, in_=in_), reads, writes, cost=ecost(eng, nfree(out), 0.7))

    def mmg(out, pairs, reads, writes, first_start=True, **mkw):
        n = len(pairs)
        ncol = max(nfree(out), 64)

        def fn(pe):
            ins = None
            for i, (l, r) in enumerate(pairs):
                ins = pe.matmul(out, lhsT=l, rhs=r, start=(first_start and i == 0), stop=(i == n - 1), **mkw)
            return ins
        c4 = 4.0 if pairs[0][0].dtype == F32 else 1.0
        P.op("pe", fn, reads, writes, cost=n * (c4 * ncol / 2200.0 + 0.027))

    def transposes(outs_ins, idn, reads, writes):
        def fn(pe):
            ins = None
            for o, s_ in outs_ins:
                ins = pe.transpose(out=o, in_=s_, identity=idn)
            return ins
        P.op("pe", fn, reads, writes, cost=0.07 * len(outs_ins))

    def rstd_from(ssq, lnv, rstd, n, key_ss, key_ln, key_r):
        kss = key_ss if isinstance(key_ss, list) else [key_ss]
        act(lnv, ssq, AF.Ln, kss, [key_ln], scale=1.0 / n, bias=EPS)
        act(rstd, lnv, AF.Exp, [key_ln], [key_r], scale=-0.5)

    A = ExitStack()
    identf = P.sb(A, "ident_f", [128, 128], F32)
    P.dma("sp", "cst0", identf[:], ident_d[:, :], writes=["identf"])
    P.dma("sp", "cst1", maskT[:], maskT_d[:, :], writes=["maskT"])
    P.dma("sp", "cst2", cw[:], cw_d[:, :, :], writes=["cw"])
    P.dma("sp", "cst3", pvec[:], pvec_d[:, :, :], writes=["pvec"])
    P.dma("sp", "cst4", hgT[:], g_glaT[:, :], writes=["hgT"])
    P.dma("sp", "cst5", gfin[:], g_final_bc[:, :], writes=["gfin"])
    P.dma("sp", "cst6", nbg2[:], b_g2T[:, :], writes=["nbg2"])
    P.dma("pool", "cst7", wgx[:], wgx_d[:, :, :], writes=["wgx"])
    P.dma("pool", "cst8", wga[:], wga_d[:, :, :], writes=["wga"])
    P.dma("pool", "cst9", wg2[:], w_g2[:, :], writes=["wg2"])
    cp("pool", ident[:], identf[:], ["identf"], ["ident"])
    P.op("pool", lambda g: g.memset(ones[:], 1.0), [], ["ones"])
    P.op("pool", lambda g: g.memset(lnh[:], math.log(0.5)), [], ["lnh"])
    P.op("pool", lambda g: g.memset(hist[:], 0.0), [], ["hist"])
    P.op("pool", lambda g: g.memset(hcarry[:], 0.0), [], ["hcarry"])
    P.op("pool", lambda g: g.memset(S_b[:], 0.0), [], ["S_b"])
    ts("pool", hgT[:], hgT[:], 0.5, None, ALU.mult, None, ["hgT"], ["hgT"])
    ts("pool", nbg2[:], nbg2[:], -1.0, None, ALU.mult, None, ["nbg2"], ["nbg2"])
    ts("pool", hbg[:], pvec[:, :, 1:3], 0.5, None, ALU.mult, None, ["pvec"], ["hbg"])
    act(c8[:], pvec[:, :, 3], AF.Exp, ["pvec"], ["c8"], scale=-1.0)
    act(c8[:], c8[:], AF.Ln, ["c8"], ["c8"], bias=1.0)
    ts("dve", hc8[:], c8[:], -4.0, None, ALU.mult, None, ["c8"], ["hc8"])
    ts("dve", c8[:], c8[:], -8.0, None, ALU.mult, None, ["c8"], ["c8"])

    cTs = P.sb(A, "cTs", [128, 8, NTOK], F32)
    tcT = P.sb(A, "tcT", [128, 8, NTOK], F32)
    scT = P.sb(A, "scT", [128, 8, NTOK], BF16)
    bchunk = [P.sb(A, f"bchunk{i}", [NTOK, 512], F32) for i in range(2)]
    ada_tok = P.sb(A, "ada_tok", [NTOK, 3 * D], F32)
    adaT = P.sb(A, "adaT", [128, 16], F32)
    badaT = P.sb(A, "badaT", [128, 16], F32)
    gnT = P.sb(A, "gnT", [128, 8], F32)
    sel = P.sb(A, "sel", [NTOK, 128], F32)
    eye16 = P.sb(A, "eye16", [128, NS, NS], F32)

    P.dma("sp", "a0", cTs[:], cT.rearrange("(k p) n -> p k n", p=128), writes=["cTs"])
    P.dma("sp", "a1", badaT[:], b_adaT[:, :], writes=["badaT"])
    P.dma("sp", "a2", gnT[:], g_normT[:, :], writes=["gnT"])
    P.dma("sp", "a3", sel[:], sel_d[:, :], writes=["sel"])
    P.dma("sp", "a4", eye16[:], eye16_d.rearrange("p (a b) -> p a b", a=NS), writes=["eye16"])
    act(tcT[:], cTs[:], AF.Tanh, ["cTs"], ["tcT"], scale=0.5)
    stt(tcT[:], tcT[:], 1.0, cTs[:], ALU.add, ALU.mult, ["tcT", "cTs"], ["tcT"])
    ts("dve", scT[:], tcT[:], 0.5, None, ALU.mult, None, ["tcT"], ["scT"])

    def wslot(cc):
        if cc < 4:
            return w_out_bf[:, 4 * cc:4 * cc + 4, :].rearrange("p a (b c) -> p (a b) c", b=2), ("wout", cc), f"wout{cc}"
        g = (9, 3)[cc - 4]
        a, b = WIN_GROUPS[g]
        return w_in_bf[:, :, a:b], ("win", g), f"win{g}"

    kT, bT = bank(reserve=True)
    for cc in range(6):
        wb, wk, wsem = wslot(cc)
        bc = bchunk[cc % 2]; bk = f"bchunk{cc % 2}"
        c0 = cc * 512
        P.dma("pool", wsem, wb, w_ada[:, c0:c0 + 512].rearrange("(k p) n -> p k n", p=128), writes=[wk], nbytes=2 << 20)
        P.dma("sp", bk, bc[:], b_ada_tok[:, c0:c0 + 512], writes=[bk])
        kb, bb = bank()
        mmg(bb[0:NTOK, 0:512], [(scT[:, k, :], wb[:, k, :]) for k in range(8)], ["scT", wk], [kb])
        tt("dve", ada_tok[:, c0:c0 + 512], bb[0:NTOK, 0:512], bc[:], ALU.add, [kb, bk], ["ada_tok"])
        if cc < 4:
            for f in range(4):
                ft = cc * 4 + f
                mmg(bT[:, ft:ft + 1], [(wb[:, k, f * 128:(f + 1) * 128], scT[:, k, NS:NS + 1]) for k in range(8)], ["scT", wk], [kT])
    tt("dve", adaT[:], bT[:, 0:16], badaT[:], ALU.add, [kT, "badaT"], ["adaT"])
    reserved.clear()
    cp("dve", shiftp[:], adaT[:, 0:8], ["adaT"], ["shiftp"])
    stt(geffp[:], adaT[:, 8:16], 1.0, gnT[:], ALU.add, ALU.mult, ["adaT", "gnT"], ["geffp"])
    for half in range(2):
        kb, bb = bank()
        mmg(bb[:, :], [(sel[:, :], ada_tok[:, 2 * D + half * 512: 2 * D + (half + 1) * 512])], ["sel", "ada_tok"], [kb])
        cp("dve", ygla[:, half * 512:(half + 1) * 512], bb[:, :], [kb], ["ygla"])

    def load_win(g, reads=()):
        a, b = WIN_GROUPS[g]
        P.dma("pool", f"win{g}", w_in_bf[:, :, a:b], w_in[:, a:b].rearrange("(k p) n -> p k n", p=128), reads=list(reads), writes=[("win", g)], nbytes=(b - a) * 4096)

    def load_wout(g, reads=()):
        P.dma("pool", f"wout{g}", w_out_bf[:, 4 * g:4 * g + 4, :],
              w_out[g * 512:(g + 1) * 512, :].rearrange("(k p) n -> p k n", p=128), reads=list(reads), writes=[("wout", g)], nbytes=2 << 20)

    for g in (4, 5, 6, 7, 8):
        load_win(g)
    late = [("win", 0), ("win", 1), ("win", 2), ("win", 9), ("win", 3), ("wout", 0), ("wout", 1), ("wout", 2), ("wout", 3)]
    WOUT = [("wout", g) for g in range(4)]

    def scale_wout():
        for kt in range(8, 16):
            act(w_out_bf[:, kt, :], w_out_bf[:, kt, :], AF.Copy, [("wout", kt // 4), "hgT"], [("wout", kt // 4)], scale=hgT[:, kt - 8:kt - 7])
        for g in range(2):
            act(w_out_bf[:, 4 * g:4 * g + 4, :], w_out_bf[:, 4 * g:4 * g + 4, :], AF.Copy, [("wout", g)], [("wout", g)], scale=0.5)

    xs_t = P.sb(A, "xs_t", [NS, D], F32)
    gnbc = P.sb(A, "gnbc", [NS, D], F32)
    s_junk = P.sb(A, "s_junk", [NS, D], BF16)
    s_ss = P.sb(A, "s_ss", [NS, 1], F32)
    s_ln = P.sb(A, "s_ln", [NS, 1], F32)
    s_rstd = P.sb(A, "s_rstd", [NS, 1], F32)
    s_xn = P.sb(A, "s_xn", [NS, D], F32)
    s_hn = P.sb(A, "s_hn", [NS, D], BF16)
    hnTs = P.sb(A, "hnTs", [128, 8, NS], BF16)
    zTs = P.sb(A, "zTs", [128, 20, NS], F32)
    glrTs = P.sb(A, "glrTs", [16, NS], BF16)
    k_tok = P.sb(A, "k_tok", [NS, 512], F32)
    v_toks = P.sb(A, "v_toks", [NS, D], BF16)
    Ws = P.sb(A, "Ws", [NS, D], BF16)
    tzs = gnbc
    h0T = P.sb(A, "h0T", [128, 8, NS], F32)
    c0T = P.sb(A, "c0T", [128, 8, 3, NS], F32)
    csT_t = P.sb(A, "csT_t", [128, 8, 3, NS], F32)

    P.dma("sp", "s0", xs_t[:], xs[:, :], writes=["xs_t"])
    P.dma("sp", "s1", gnbc[:], g_norm_bc[:, :], writes=["gnbc"])
    P.dma("sp", "s2", h0T[:], h0T_d[:, :, :], writes=["h0T"])
    P.dma("sp", "s3", c0T[:], c0T_d[:, :, :, :], writes=["c0T"])
    act(s_junk[:], xs_t[:], AF.Square, ["xs_t"], ["s_junk", "s_ss"], accum=s_ss[:])
    rstd_from(s_ss[:], s_ln[:], s_rstd[:], D, "s_ss", "s_ln", "s_rstd")
    act(s_xn[:], xs_t[:], AF.Copy, ["xs_t", "s_rstd"], ["s_xn"], scale=s_rstd[:])
    stt(gnbc[:], ada_tok[0:NS, D:2 * D], 1.0, gnbc[:], ALU.add, ALU.mult, ["ada_tok", "gnbc"], ["gnbc"])
    tt("dve", s_xn[:], s_xn[:], gnbc[:], ALU.mult, ["s_xn", "gnbc"], ["s_xn"])
    tt("dve", s_hn[:], s_xn[:], ada_tok[0:NS, 0:D], ALU.add, ["s_xn", "ada_tok"], ["s_hn"])
    kb, bb = bank()
    bbv = bb[:].bitcast(BF16)
    transposes([(bbv[:, k * NS:(k + 1) * NS], s_hn[:, k * 128:(k + 1) * 128]) for k in range(8)], ident[0:NS, 0:NS], ["s_hn", "ident"], [kb])
    cp("dve", hnTs[:], bbv[:, 0:8 * NS].rearrange("p (k n) -> p k n", k=8), [kb], ["hnTs"])

    kb, bb = bank()
    for i, col in enumerate([QO + j * 128 for j in range(4)]):
        mmg(bb[:, i * NS:(i + 1) * NS], [(w_in_bf[:, k, col:col + 128], hnTs[:, k, :]) for k in range(8)],
            ["hnTs"] + win_keys(col, col + 128), [kb])
    cp("dve", zTs[:, 16:20, :], bb[:, 0:4 * NS].rearrange("p (f n) -> p f n", f=4), [kb], ["zTq"])
    kb, bb = bank()
    mmg(bb[0:16, 0:NS], [(w_in_bf[:, k, GLR:GLR + 16], hnTs[:, k, :]) for k in range(8)], ["hnTs"] + win_keys(GLR, GLR + 16), [kb])
    cp("dve", glrTs[:], bb[0:16, 0:NS], [kb], ["glrTs"])
    kb, bb = bank()
    mmg(bb[0:NS, :], [(hnTs[:, k, :], w_in_bf[:, k, KO:KO + 512]) for k in range(8)], ["hnTs"] + win_keys(KO, KO + 512), [kb])
    cp("dve", k_tok[:], bb[0:NS, :], [kb], ["k_tok"])
    for half in range(2):
        kb, bb = bank()
        c0 = VO + half * 512
        mmg(bb[0:NS, :], [(hnTs[:, k, :], w_in_bf[:, k, c0:c0 + 512]) for k in range(8)], ["hnTs"] + win_keys(c0, c0 + 512), [kb])
        cp("dve", v_toks[:, half * 512:(half + 1) * 512], bb[0:NS, :], [kb], ["v_toks"])

    al_s = P.sb(A, "al_s", [128, 4, NS], F32)
    qmask = P.sb(A, "qmask", [128, 4, NS, NS], BF16)
    Sb16 = [P.sb(A, f"Sb16_{i}", [128, 4, 256], BF16) for i in range(2)]
    qT_s = P.sb(A, "qT_s", [128, 4, NS], F32)
    kmb = [P.sb(A, f"kmb{i}", [NS, 512], BF16) for i in range(2)]
    qkp = P.sb(A, "qkp", [NS, 512], F32)
    qk_s = P.sb(A, "qk_s", [NS, 4], F32)
    Sbuf = [P.sb(A, "Sbuf0", [128, 4, 256], F32), S_f, P.sb(A, "Sbuf2", [128, 4, 256], F32),
            P.sb(A, "Sbuf3", [128, 4, 256], F32)]
    NSB = len(Sbuf)
    kb, bb = bank()
    for h in range(4):
        mmg(bb[:, h * NS:(h + 1) * NS], [(wg2[:, h * 128:(h + 1) * 128], glrTs[:, :])], ["wg2", "glrTs"], [kb])
    tt("dve", al_s[:], bb[:, 0:4 * NS].rearrange("p (h n) -> p h n", h=4), nbg2[:].unsqueeze(2).to_broadcast([128, 4, NS]),
       ALU.subtract, [kb, "nbg2"], ["al_s"])
    act(al_s[:], al_s[:], AF.Exp, ["al_s"], ["al_s"], scale=-1.0)
    act(al_s[:], al_s[:], AF.Ln, ["al_s"], ["al_s"], bias=1.0)
    act(al_s[:], al_s[:], AF.Exp, ["al_s"], ["al_s"], scale=-1.0 / 16)
    stt(qT_s[:], zTs[:, 16:20, :], 128 ** -0.5, al_s[:], ALU.mult, ALU.mult, ["zTq", "al_s"], ["qT_s"])
    for h in range(4):
        tt("dve", qmask[:, h, :, :], qT_s[:, h, :].unsqueeze(2).to_broadcast([128, NS, NS]), eye16[:], ALU.mult,
           ["qT_s", "eye16"], ["qmask"])
    kb, bb = bank()
    mmg(bb[0:NS, :], [(hnTs[:, k, :], w_in_bf[:, k, QO:QO + 512]) for k in range(8)], ["hnTs"] + win_keys(QO, QO + 512), [kb])
    tt("dve", qkp[:], bb[0:NS, :], k_tok[:], ALU.mult, [kb, "k_tok"], ["qkp"])
    P.op("dve", lambda v: v.tensor_reduce(out=qk_s[:], in_=qkp[:].rearrange("p (h k) -> p h k", h=4), axis=mybir.AxisListType.X, op=ALU.add),
         ["qkp"], ["qk_s"], cost=0.7)
    ts("dve", qk_s[:], qk_s[:], 128 ** -0.5, None, ALU.mult, None, ["qk_s"], ["qk_s"])
    ko0, bo0 = bank(reserve=True)
    ko1, bo1 = bank(reserve=True)
    obank = [(ko0, bo0), (ko0, bo0), (ko1, bo1), (ko1, bo1)]
    for b in range(NS):
        sbk = f"Sbuf{b % NSB}"; St = Sbuf[b % NSB]
        P.dma("sp", sbk, St[:], S0_d[b].rearrange("h k v -> k h v"), writes=[sbk], nbytes=512 * 1024)
        km = kmb[b % 2]; kmk = f"kmb{b % 2}"
        ts("dve", km[:], k_tok[:], identf[0:NS, b:b + 1], None, ALU.mult, None, ["k_tok", "identf"], [kmk])
        Sh = Sb16[b % 2]; shk = f"Sb16_{b % 2}"
        cp("act", Sh[:], St[:], [sbk], [shk])
        for h in range(4):
            ko, bo = obank[h]
            P.op("pe", lambda pe, h=h, b=b, bo=bo, Sh=Sh: pe.matmul(bo[0:NS, (h % 2) * 256:(h % 2 + 1) * 256], lhsT=qmask[:, h, b, :], rhs=Sh[:, h, :],
                                                                      start=(b == 0 and h % 2 == 0), stop=(b == NS - 1), skip_group_check=True),
                 ["qmask", shk, ko], [ko], cost=0.12)
        for hp in range(2):
            kb, bb = bank()
            for hh in range(2):
                h = hp * 2 + hh
                mmg(bb[:, hh * 256:(hh + 1) * 256], [(km[:, h * 128:(h + 1) * 128], v_toks[:, h * 256:(h + 1) * 256])],
                    [kmk, "v_toks"], [kb])
            for hh in range(2):
                h = hp * 2 + hh
                stt(St[:, h, :], St[:, h, :], al_s[:, h, b:b + 1], bb[:, hh * 256:(hh + 1) * 256], ALU.mult, ALU.add,
                    [sbk, "al_s", kb], [sbk, ("tick", b)])
        P.dma("sp", sbk, ss_o[b].rearrange("h k v -> k h v"), St[:], reads=[sbk], nbytes=512 * 1024)
        if b < len(late):
            kind, g = late[b]
            (load_win if kind == "win" else load_wout)(g, reads=[("tick", b)])
    out_dma_keys += [f"Sbuf{i}" for i in range(NSB)]
    scale_wout()
    for half in range(2):
        kb, bb = bank()
        c0 = ZG + half * 512
        sl = slice(half * 512, (half + 1) * 512)
        mmg(bb[0:NS, :], [(hnTs[:, k, :], w_in_bf[:, k, c0:c0 + 512]) for k in range(8)], ["hnTs"] + win_keys(c0, c0 + 512), [kb])
        act(tzs[:, sl], bb[0:NS, :], AF.Tanh, [kb], ["gnbc"], scale=0.5)
        stt(Ws[:, sl], tzs[:, sl], 1.0, bb[0:NS, :], ALU.add, ALU.mult, ["gnbc", kb], ["Ws"])

    kb, bb = bank()
    for i, col in enumerate([XA + j * 128 for j in range(8)] + [ZA + j * 128 for j in range(8)]):
        mmg(bb[:, i * NS:(i + 1) * NS], [(w_in_bf[:, k, col:col + 128], hnTs[:, k, :]) for k in range(8)],
            ["hnTs"] + win_keys(col, col + 128), [kb])
    cp("dve", zTs[:, 0:16, :], bb[:, 0:16 * NS].rearrange("p (f n) -> p f n", f=16), [kb], ["zTs"])
    def bc3(ap2, n=NS):
        return ap2.unsqueeze(2).to_broadcast([128, 8, n])

    xc_s = P.sb(A, "xc_s", [128, 8, NS], F32)
    tmp_s = P.sb(A, "tmp_s", [128, 8, NS], F32)
    xcb_s = P.sb(A, "xcb_s", [128, 8, NS], BF16)
    tx_s = P.sb(A, "tx_s", [128, 8, NS], F32)
    tg_s = P.sb(A, "tg_s", [128, 8, NS], F32)
    a_s = P.sb(A, "a_s", [128, 8, NS], F32)
    a2_s = P.sb(A, "a2_s", [128, 8, NS], F32)
    h_s = P.sb(A, "h_s", [128, 8, NS], F32)
    tz_s = P.sb(A, "tz_s", [128, 8, NS], F32)
    yTs = P.sb(A, "yTs", [128, 16, NS], BF16)
    xaTs = zTs[:, 0:8, :]
    zaTs = zTs[:, 8:16, :]
    tt("dve", xc_s[:], xaTs, bc3(cw[:, :, 3]), ALU.mult, ["zTs", "cw"], ["xc_s"])
    for jj in range(3):
        tt("dve", tmp_s[:], c0T[:, :, jj, :], bc3(cw[:, :, jj]), ALU.mult, ["c0T", "cw"], ["tmp_s"])
        tt("dve", xc_s[:], xc_s[:], tmp_s[:], ALU.add, ["xc_s", "tmp_s"], ["xc_s"])
    tt("dve", xc_s[:], xc_s[:], bc3(pvec[:, :, 0]), ALU.add, ["xc_s", "pvec"], ["xc_s"])
    cp("dve", xcb_s[:], xc_s[:], ["xc_s"], ["xcb_s"])
    kb, bb = bank()
    for j in range(8):
        mmg(bb[:, j * NS:(j + 1) * NS], [(wgx[:, j, :], xcb_s[:, j, :])], ["wgx", "xcb_s"], [kb])
        mmg(bb[:, 128 + j * NS:128 + (j + 1) * NS], [(wga[:, j, :], xcb_s[:, j, :])], ["wga", "xcb_s"], [kb])
    v3 = lambda ap: ap.rearrange("p (j n) -> p j n", j=8)
    stt(tx_s[:], v3(bb[:, 0:128]), 0.5, bc3(hbg[:, :, 0]), ALU.mult, ALU.add, [kb, "hbg"], ["tx_s"])
    stt(tg_s[:], v3(bb[:, 128:256]), 0.5, bc3(hbg[:, :, 1]), ALU.mult, ALU.add, [kb, "hbg"], ["tg_s"])
    act(tx_s[:], tx_s[:], AF.Tanh, ["tx_s"], ["tx_s"])
    act(tg_s[:], tg_s[:], AF.Tanh, ["tg_s"], ["tg_s"])
    stt(tg_s[:], tg_s[:], 1.0, bc3(hc8[:]), ALU.add, ALU.mult, ["tg_s", "hc8"], ["tg_s"])
    act(a_s[:], tg_s[:], AF.Exp, ["tg_s"], ["a_s"])
    act(a2_s[:], tg_s[:], AF.Exp, ["tg_s"], ["a2_s"], scale=2.0)
    act(tz_s[:], zaTs, AF.Tanh, ["zTs"], ["tz_s"], scale=0.5)
    act(a2_s[:], a2_s[:], AF.Ln, ["a2_s"], ["a2_s"], scale=-1.0, bias=1.0)
    act(a2_s[:], a2_s[:], AF.Exp, ["a2_s"], ["a2_s"], scale=0.5)
    stt(tx_s[:], tx_s[:], 1.0, xc_s[:], ALU.add, ALU.mult, ["tx_s", "xc_s"], ["tx_s"])
    stt(tx_s[:], tx_s[:], 0.5, a2_s[:], ALU.mult, ALU.mult, ["tx_s", "a2_s"], ["tx_s"])
    tt("dve", h_s[:], a_s[:], h0T[:], ALU.mult, ["a_s", "h0T"], ["h_s"])
    tt("dve", h_s[:], h_s[:], tx_s[:], ALU.add, ["h_s", "tx_s"], ["h_s"])
    P.dma("sp", "o_hsT", hsT[:, :, :], h_s[:], reads=["h_s"]); out_dma_keys.append("o_hsT")
    stt(tz_s[:], tz_s[:], 1.0, zaTs, ALU.add, ALU.mult, ["tz_s", "zTs"], ["tz_s"])
    tt("dve", yTs[:, 0:8, :], h_s[:], tz_s[:], ALU.mult, ["h_s", "tz_s"], ["yTs"])
    cp("pool", csT_t[:, :, 0:2, :], c0T[:, :, 1:3, :], ["c0T"], ["csT_t"])
    cp("pool", csT_t[:, :, 2, :], xaTs, ["zTs", "csT_t"], ["csT_t"])
    P.dma("sp", "o_csT", csT[:, :, :, :], csT_t[:], reads=["csT_t"]); out_dma_keys.append("o_csT")


    so_ss = P.sb(A, "so_ss", [NS, 4], F32)
    so_ln = P.sb(A, "so_ln", [NS, 4], F32)
    so_r = P.sb(A, "so_r", [NS, 4], F32)
    ygs = P.sb(A, "ygs", [NS, D], BF16)
    o_sb = s_xn
    for h in range(4):
        ko, bo = obank[h]
        hs = slice(h * 256, (h + 1) * 256)
        stt(o_sb[:, hs], v_toks[:, hs], qk_s[:, h:h + 1], bo[0:NS, (h % 2) * 256:(h % 2 + 1) * 256], ALU.mult, ALU.add,
            ["v_toks", "qk_s", ko], ["s_xn"])
        act(ygs[:, 0:256], o_sb[:, hs], AF.Square, ["s_xn"], ["ygs", "so_ss"], accum=so_ss[:, h:h + 1])
    rstd_from(so_ss[:], so_ln[:], so_r[:], 256, "so_ss", "so_ln", "so_r")
    for h in range(4):
        hs = slice(h * 256, (h + 1) * 256)
        stt(ygs[:, hs], o_sb[:, hs], so_r[:, h:h + 1], Ws[:, hs], ALU.mult, ALU.mult, ["s_xn", "so_r", "Ws"], ["ygs"])
    reserved.clear()
    kb, bb = bank()
    bbv = bb[:].bitcast(BF16)
    transposes([(bbv[:, k * NS:(k + 1) * NS], ygs[:, k * 128:(k + 1) * 128]) for k in range(8)], ident[0:NS, 0:NS], ["ygs", "ident"], [kb])
    cp("dve", yTs[:, 8:16, :], bbv[:, 0:8 * NS].rearrange("p (k n) -> p k n", k=8), [kb], ["yTs"])
    r_s = s_xn
    f_ss = P.sb(A, "f_ss", [NS, 1], F32)
    f_ln = P.sb(A, "f_ln", [NS, 1], F32)
    f_r = P.sb(A, "f_r", [NS, 1], F32)
    for half in range(2):
        kb, bb = bank()
        sl = slice(half * 512, (half + 1) * 512)
        mmg(bb[0:NS, :], [(yTs[:, kt, :], w_out_bf[:, kt, sl]) for kt in range(16)], ["yTs"] + WOUT, [kb])
        tt("dve", r_s[:, sl], bb[0:NS, :], ada_tok[0:NS, 2 * D + half * 512:2 * D + (half + 1) * 512], ALU.mult, [kb, "ada_tok"], ["s_xn"])
    tt("dve", r_s[:], r_s[:], xs_t[:], ALU.add, ["s_xn", "xs_t"], ["s_xn"])
    act(s_junk[:], r_s[:], AF.Square, ["s_xn"], ["s_junk", "f_ss"], accum=f_ss[:])
    rstd_from(f_ss[:], f_ln[:], f_r[:], D, "f_ss", "f_ln", "f_r")
    stt(r_s[:], r_s[:], f_r[:], gfin[0:NS, :], ALU.mult, ALU.mult, ["s_xn", "f_r", "gfin"], ["s_xn"])
    P.dma("sp", "o_ys", ys[:, :], r_s[:], reads=["s_xn"]); out_dma_keys.append("o_ys")

    P.barrier()
    A.close()

    B = ExitStack()
    P.op("pool", lambda g: g.memset(S_f[:], 0.0), [], [("S_f", h) for h in range(4)])
    for kt in range(16):
        tt("pool" if kt % 8 in (2, 5, 7) else "dve", w_out_bf[:, kt, :], w_out_bf[:, kt, :], ygla[:, :], ALU.mult,
           [("wout", kt // 4), "ygla"], [("wout", kt // 4)])
    hnTs2 = [P.sb(B, f"hnT{i}", [128, 8, TB], BF16) for i in range(2)]
    yTs2 = [P.sb(B, f"yT{i}", [128, 16, TB], BF16) for i in range(2)]
    v_tok = P.sb(B, "v_tok", [128, 2, D], BF16)
    x_in = P.sb(B, "x_in", [128, D], F32)
    xn = P.sb(B, "xn0", [128, D], BF16)
    rbuf = P.sb(B, "rbuf0", [128, D], F32)
    fjunk = P.sb(B, "fjunk", [128, D], BF16)
    p_ss = P.sb(B, "p_ss", [128, 2], F32); p_ln = P.sb(B, "p_ln", [128, 2], F32); p_r = P.sb(B, "p_r", [128, 2], F32)
    f_ss2 = P.sb(B, "f_ss2", [128, 2], F32); f_ln2 = P.sb(B, "f_ln2", [128, 2], F32); f_r2 = P.sb(B, "f_r2", [128, 2], F32)
    o_ss = P.sb(B, "o_ss", [128, 4], F32); o_ln = P.sb(B, "o_ln", [128, 4], F32); o_r = P.sb(B, "o_r", [128, 4], F32)
    glrT = P.sb(B, "glrT", [16, TB], BF16)
    tE = P.sb(B, "tE", [128, TB], F32)
    Pb = [P.sb(B, f"Pb{i}", [128, TB], F32) for i in range(2)]
    Pinv = [P.sb(B, f"Pinv{i}", [128, TB], F32) for i in range(2)]
    plast = P.sb(B, "plast", [128, 4, 2], F32)
    q_in = P.sb(B, "q_in", [128, 4, TB], BF16)
    k_in = P.sb(B, "k_in", [128, 4, TB], BF16)
    kdecT = P.sb(B, "kdecT", [128, TB], BF16)
    kdec_tok = P.sb(B, "kdec_tok", [128, 4, 2, 128], BF16)
    Am = P.sb(B, "Am", [128, 4, 128], BF16)
    tzg = P.sb(B, "tzg", [128, 512], F32)
    Wt = P.sb(B, "Wt", [128, D], BF16)
    LT = []
    for t in range(2):
        LT.append(dict(
            xe=P.sb(B, f"xa_ext{t}", [128, TB + 3], F32), xc=P.sb(B, f"xc{t}", [128, TB], F32), xcb=P.sb(B, f"xcb{t}", [128, TB], BF16),
            tx=P.sb(B, f"txb{t}", [128, TB], F32), tg=P.sb(B, f"tgb{t}", [128, TB], F32), gxc=P.sb(B, f"gxc{t}", [128, TB], F32),
            Aa=P.sb(B, f"Aa{t}", [128, TB], F32), A2=P.sb(B, f"A2{t}", [128, TB], F32), sz=P.sb(B, f"szb{t}", [128, TB], BF16),
            hb=P.sb(B, f"hbuf{t}", [128, TB], F32)))

    class Ring:
        def __init__(self, idxs):
            self.idxs = idxs
            self.i = 0

        def get(self):
            b = self.idxs[self.i % len(self.idxs)]
            self.i += 1
            return ("ps", b), banks[b]

    ringG = Ring([0, 1, 2])
    ringL = Ring([3, 4, 5])
    ringB = Ring([6, 7])
    XB = 512 * 1024

    def mm_fm(bb, kb, hn, hks, col, ncols):
        mmg(bb[0:ncols, 0:TB], [(w_in_bf[:, k, col:col + ncols], hn[:, k, :]) for k in range(8)], list(hks) + win_keys(col, col + ncols), [kb])

    def prenorm(blk):
        t0 = blk * TB
        hn = hnTs2[blk % 2]; hk = f"hnT{blk % 2}"
        for i in range(2):
            P.dma("sp", "x_in", x_in[:], xp[t0 + i * 128:t0 + (i + 1) * 128, :], writes=["x_in"], nbytes=XB)
            act(xn[:], x_in[:], AF.Square, ["x_in"], ["xn0", ("p_ss", i)], accum=p_ss[:, i:i + 1])
            act(p_ln[:, i:i + 1], p_ss[:, i:i + 1], AF.Ln, [("p_ss", i)], [("p_ln", i)], scale=1.0 / D, bias=EPS)
            act(p_r[:, i:i + 1], p_ln[:, i:i + 1], AF.Exp, [("p_ln", i)], [("p_r", i)], scale=-0.5)
            act(xn[:], x_in[:], AF.Copy, ["x_in", ("p_r", i)], ["xn0"], scale=p_r[:, i:i + 1])
            kb, bb = ringB.get()
            bbv = bb[:].bitcast(BF16)
            transposes([(bbv[:, k * 128:(k + 1) * 128], xn[:, k * 128:(k + 1) * 128]) for k in range(8)], ident[:], ["xn0", "ident"], [kb])
            hv = hn[:, :, i * 128:(i + 1) * 128]
            tt("dve", hv, bbv[:, :].rearrange("p (k t) -> p k t", k=8), geffp[:, :].unsqueeze(2).to_broadcast([128, 8, 128]), ALU.mult,
               [kb, "geffp"], [(hk, i)])
            tt("dve", hv, hv, shiftp[:, :].unsqueeze(2).to_broadcast([128, 8, 128]), ALU.add, [(hk, i), "shiftp"], [(hk, i)])

    def gla(blk):
        hn = hnTs2[blk % 2]; hk = f"hnT{blk % 2}"
        HK = [(hk, 0), (hk, 1)]
        yT = yTs2[blk % 2]; yk = f"yT{blk % 2}"
        for i in range(2):
            for half in range(2):
                kb, bb = ringG.get()
                c0 = VO + half * 512
                mmg(bb[:, :], [(hn[:, k, i * 128:(i + 1) * 128], w_in_bf[:, k, c0:c0 + 512]) for k in range(8)], [(hk, i)] + win_keys(c0, c0 + 512), [kb])
                cp("act", v_tok[:, i, half * 512:(half + 1) * 512], bb[:, :], [kb], [("v_tok", i)])
        kb, bb = ringG.get()
        mmg(bb[0:16, 0:TB], [(w_in_bf[:, k, GLR:GLR + 16], hn[:, k, :]) for k in range(8)], HK + win_keys(GLR, GLR + 16), [kb])
        cp("act", glrT[:], bb[0:16, 0:TB], [kb], ["glrT"])
        for h in range(4):
            Pp = Pb[h % 2]; pk = f"Pb{h % 2}"; Pi = Pinv[h % 2]; pik = f"Pinv{h % 2}"
            kb, bb = ringG.get()
            mmg(bb[:, 0:TB], [(wg2[:, h * 128:(h + 1) * 128], glrT[:, :])], ["wg2", "glrT"], [kb])
            act(tE[:], bb[:, 0:TB], AF.Exp, [kb, "nbg2"], ["tE"], scale=-1.0, bias=nbg2[:, h:h + 1])
            act(tE[:], tE[:], AF.Ln, ["tE"], ["tE"], bias=1.0)
            for c in range(2):
                cs = slice(c * 128, (c + 1) * 128)
                scan(Pp[:, cs], ones[:, :], tE[:, cs], 0.0, ["tE", "ones"], [(pk, c)])
            act(Pi[:], Pp[:], AF.Exp, [(pk, 0), (pk, 1)], [pik], scale=1.0 / 16)
            act(Pp[:], Pp[:], AF.Exp, [(pk, 0), (pk, 1)], [(pk, 0), (pk, 1)], scale=-1.0 / 16)
            cp("pool", plast[:, h, :], Pp[:].rearrange("p (c t) -> p c t", c=2)[:, :, 127], [(pk, 0), (pk, 1)], [("plast", h)])
            kb, bb = ringG.get()
            mm_fm(bb, kb, hn, HK, QO + h * 128, 128)
            stt(q_in[:, h, :], bb[:, 0:TB], 128 ** -0.5, Pp[:], ALU.mult, ALU.mult, [kb, (pk, 0), (pk, 1)], [("q_in", h)])
            kb, bb = ringG.get()
            mm_fm(bb, kb, hn, HK, KO + h * 128, 128)
            tt("dve", k_in[:, h, :], bb[:, 0:TB], Pi[:], ALU.mult, [kb, pik], [("k_in", h)])
            for c in range(2):
                cs = slice(c * 128, (c + 1) * 128)
                stt(kdecT[:, cs], bb[:, cs], plast[:, h, c:c + 1], Pi[:, cs], ALU.mult, ALU.mult, [kb, ("plast", h), pik], [("kdecT", c)])
            kb2, bb2 = ringG.get()
            bbv = bb2[:].bitcast(BF16)
            transposes([(bbv[:, c * 128:(c + 1) * 128], kdecT[:, c * 128:(c + 1) * 128]) for c in range(2)], ident[:],
                       [("kdecT", 0), ("kdecT", 1), "ident"], [kb2])
            cp("act", kdec_tok[:, h, :, :], bbv[:, 0:256].rearrange("p (c k) -> p c k", c=2), [kb2], [("kdec_tok", h)])
        QK = [("q_in", h) for h in range(4)] + [("k_in", h) for h in range(4)]
        for c in range(2):
            cs = slice(c * 128, (c + 1) * 128)
            for half in range(2):
                kb, bb = ringG.get()
                c0 = ZG + half * 512
                mmg(bb[:, :], [(hn[:, k, cs], w_in_bf[:, k, c0:c0 + 512]) for k in range(8)], [(hk, c)] + win_keys(c0, c0 + 512), [kb])
                act(tzg[:], bb[:, :], AF.Tanh, [kb], ["tzg"], scale=0.5)
                stt(Wt[:, half * 512:(half + 1) * 512], tzg[:], 1.0, bb[:, :], ALU.add, ALU.mult, ["tzg", kb], [("Wt", half)])
            kA, bA = ringG.get()
            for h in range(4):
                mmg(bA[:, h * 128:(h + 1) * 128], [(k_in[:, h, cs], q_in[:, h, cs])], QK, [kA])
            tt("dve", Am[:], bA[:, :].rearrange("p (h t) -> p h t", h=4), maskT[:].unsqueeze(1).to_broadcast([128, 4, 128]), ALU.mult,
               [kA, "maskT"], ["Am"])
            ksu = [ringG.get(), ringG.get()]
            for h in range(4):
                kk, bs = ksu[h // 2]
                mmg(bs[:, (h % 2) * 256:(h % 2 + 1) * 256], [(kdec_tok[:, h, c, :], v_tok[:, c, h * 256:(h + 1) * 256])], [("kdec_tok", h), ("v_tok", c)], [kk])
            for h in range(4):
                kk, bs = ksu[h // 2]
                stt(S_f[:, h, :], S_f[:, h, :], plast[:, h, c:c + 1], bs[:, (h % 2) * 256:(h % 2 + 1) * 256], ALU.mult, ALU.add,
                    [("S_f", h), ("plast", h), kk], [("S_f", h)])
            ko = [ringG.get(), ringG.get()]
            for h in range(4):
                kk, bo = ko[h // 2]
                mmg(bo[:, (h % 2) * 256:(h % 2 + 1) * 256],
                    [(Am[:, h, :], v_tok[:, c, h * 256:(h + 1) * 256]), (q_in[:, h, cs], S_b[:, h, :])], ["Am", ("v_tok", c), ("q_in", h), ("S_b", h)], [kk])
            for h in range(4):
                cp("pool", S_b[:, h, :], S_f[:, h, :], [("S_f", h)], [("S_b", h)])
            junk_o = fjunk[:, 0:256]
            for h in range(4):
                kk, bo = ko[h // 2]
                act(junk_o, bo[:, (h % 2) * 256:(h % 2 + 1) * 256], AF.Square, [kk], ["fjunk", "o_ss"], accum=o_ss[:, h:h + 1])
            rstd_from(o_ss[:], o_ln[:], o_r[:], 256, "o_ss", "o_ln", "o_r")
            for h in range(4):
                kk, bo = ko[h // 2]
                stt(ygla[:, h * 256:(h + 1) * 256], bo[:, (h % 2) * 256:(h % 2 + 1) * 256], o_r[:, h:h + 1], Wt[:, h * 256:(h + 1) * 256],
                    ALU.mult, ALU.mult, [kk, "o_r", ("Wt", h // 2)], ["ygla"])
            kb, bb = ringG.get()
            bbv = bb[:].bitcast(BF16)
            transposes([(bbv[:, k * 128:(k + 1) * 128], ygla[:, k * 128:(k + 1) * 128]) for k in range(8)], ident[:], ["ygla", "ident"], [kb])
            cp("act", yT[:, 8:16, cs], bbv[:, :].rearrange("p (k t) -> p k t", k=8), [kb], [(yk, "g", c)])

    def lru(blk, j):
        t = j % 2
        hn = hnTs2[blk % 2]; hk = f"hnT{blk % 2}"
        HK = [(hk, 0), (hk, 1)]
        yT = yTs2[blk % 2]; yk = f"yT{blk % 2}"
        L = LT[t]
        xe, xc, xcb, tx, tg, gxc, Aa, A2, sz, hb = (L[k] for k in ("xe", "xc", "xcb", "tx", "tg", "gxc", "Aa", "A2", "sz", "hb"))
        K = lambda n: f"{n}{t}"
        kb, bb = ringL.get()
        mm_fm(bb, kb, hn, HK, XA + j * 128, 128)
        cp("pool", xe[:, 0:3], hist[:, j, :], [("hist", j)], [K("xeh")])
        cp("act", xe[:, 3:TB + 3], bb[:, 0:TB], [kb], [K("xe")])
        cp("pool", hist[:, j, :], xe[:, TB:TB + 3], [K("xe")], [("hist", j)])
        ts("dve", xc[:], xe[:, 0:TB], cw[:, j, 0:1], pvec[:, j, 0:1], ALU.mult, ALU.add, [K("xe"), K("xeh"), "cw", "pvec"], [K("xc")])
        for jj in range(1, 4):
            stt(xc[:], xe[:, jj:jj + TB], cw[:, j, jj:jj + 1], xc[:], ALU.mult, ALU.add, [K("xe"), K("xeh"), "cw", K("xc")], [K("xc")])
        cp("pool", xcb[:], xc[:], [K("xc")], [K("xcb")])
        kg, bg = ringL.get()
        mmg(bg[:, 0:TB], [(wgx[:, j, :], xcb[:])], ["wgx", K("xcb")], [kg])
        mmg(bg[:, TB:2 * TB], [(wga[:, j, :], xcb[:])], ["wga", K("xcb")], [kg])
        act(tx[:], bg[:, 0:TB], AF.Tanh, [kg, "hbg"], [K("tx")], scale=0.5, bias=hbg[:, j, 0:1])
        act(tg[:], bg[:, TB:2 * TB], AF.Tanh, [kg, "hbg"], [K("tg")], scale=0.5, bias=hbg[:, j, 1:2])
        act(Aa[:], tg[:], AF.Exp, [K("tg"), "hc8"], [K("Aa")], scale=hc8[:, j:j + 1], bias=hc8[:, j:j + 1])
        act(A2[:], tg[:], AF.Exp, [K("tg"), "c8"], [K("A2")], scale=c8[:, j:j + 1], bias=c8[:, j:j + 1])
        stt(gxc[:], tx[:], 1.0, xc[:], ALU.add, ALU.mult, [K("tx"), K("xc")], [K("gxc")])
        kz, bz = ringL.get()
        mm_fm(bz, kz, hn, HK, ZA + j * 128, 128)
        act(tx[:], bz[:, 0:TB], AF.Tanh, [kz], [K("tx")], scale=0.5)
        stt(sz[:], tx[:], 1.0, bz[:, 0:TB], ALU.add, ALU.mult, [K("tx"), kz], [K("sz")])
        act(A2[:], A2[:], AF.Ln, [K("A2")], [K("A2")], scale=-1.0, bias=1.0)
        act(A2[:], A2[:], AF.Exp, [K("A2")], [K("A2")], scale=0.5, bias=lnh[:, 0:1])
        tt("pool", gxc[:], gxc[:], A2[:], ALU.mult, [K("gxc"), K("A2")], [K("gxc")])
        scan(hb[:], Aa[:], gxc[:], hcarry[:, j:j + 1], [K("Aa"), K("gxc"), ("hcarry", j)], [K("hb")])
        cp("pool", hcarry[:, j:j + 1], hb[:, TB - 1:TB], [K("hb")], [("hcarry", j)])
        tt("pool", yT[:, j, :], hb[:], sz[:], ALU.mult, [K("hb"), K("sz")], [(yk, j)])

    def outproj(blk):
        t0 = blk * TB
        yT = yTs2[blk % 2]; yk = f"yT{blk % 2}"
        YT_ALL = [(yk, j) for j in range(8)] + [(yk, "g", c) for c in range(2)]
        rb = rbuf; rk = "rbuf0"
        for i in range(2):
            P.dma("sp", rk, rb[:], xp[t0 + i * 128:t0 + (i + 1) * 128, :], writes=[rk], nbytes=XB)
            for half in range(2):
                kb, bb = ringB.get()
                sl = slice(half * 512, (half + 1) * 512)
                mmg(bb[:, :], [(yT[:, kt, i * 128:(i + 1) * 128], w_out_bf[:, kt, sl]) for kt in range(16)], YT_ALL + WOUT, [kb])
                tt("dve", rb[:, sl], bb[:, :], rb[:, sl], ALU.add, [kb, rk], [rk])
            act(fjunk[:], rb[:], AF.Square, [rk], ["fjunk", ("f_ss", i)], accum=f_ss2[:, i:i + 1])
            act(f_ln2[:, i:i + 1], f_ss2[:, i:i + 1], AF.Ln, [("f_ss", i)], [("f_ln", i)], scale=1.0 / D, bias=EPS)
            act(f_r2[:, i:i + 1], f_ln2[:, i:i + 1], AF.Exp, [("f_ln", i)], [("f_r", i)], scale=-0.5)
            stt(rb[:], rb[:], f_r2[:, i:i + 1], gfin[:], ALU.mult, ALU.mult, [rk, ("f_r", i), "gfin"], [rk])
            P.dma("sp", rk, yp[t0 + i * 128:t0 + (i + 1) * 128, :], rb[:], reads=[rk], nbytes=XB)

    prenorm(0)
    for blk in range(NB):
        gla(blk)
        for j in range(8):
            lru(blk, j)
        if blk + 1 < NB:
            prenorm(blk + 1)
        outproj(blk)

    P.dma("sp", "o_hpT", hpT[:, :], hcarry[:], reads=[("hcarry", j) for j in range(8)]); out_dma_keys.append("o_hpT")
    P.dma("sp", "o_cpT", cpT[:, :, :], hist[:], reads=[("hist", j) for j in range(8)]); out_dma_keys.append("o_cpT")
    for h in range(4):
        P.dma("sp", "o_sp", sp_o[h], S_f[:, h, :], reads=[("S_f", h)])
    out_dma_keys.append("o_sp")
    P.barrier()
    P.flush()
    B.close()
    P.keep.close()
    build_nc.sim_log = P.sim_log
    return nc


def _prep_inputs(inp):
    f = lambda a: np.ascontiguousarray(a, dtype=np.float32)
    fm8 = lambda v: f(np.asarray(v).reshape(8, 128).T)
    shared = {
        "w_ada": f(inp["w_ada"][0]),
        "b_ada_tok": f(np.tile(np.asarray(inp["b_ada"][0])[None, :], (NTOK, 1))),
        "b_adaT": f(np.asarray(inp["b_ada"][0])[:2 * D].reshape(16, 128).T),
        "g_normT": fm8(inp["g_norm"][0]),
        "g_norm_bc": f(np.tile(np.asarray(inp["g_norm"][0])[None, :], (NS, 1))),
        "w_in": f(inp["w_in"][0]),
        "cw": f(np.asarray(inp["conv_w"][0]).reshape(4, 8, 128).transpose(2, 1, 0)),
        "pvec": f(np.stack([fm8(inp["conv_b"][0]), fm8(inp["b_gate_x"][0]), fm8(inp["b_gate_a"][0]), fm8(inp["lru_lambda"][0])], axis=2)),
        "wgx": f(np.asarray(inp["w_gate_x"][0]).transpose(1, 0, 2)),
        "wga": f(np.asarray(inp["w_gate_a"][0]).transpose(1, 0, 2)),
        "w_g2": f(inp["w_gla_g2"][0]),
        "b_g2T": f(np.asarray(inp["b_gla_g2"][0]).reshape(4, 128).T),
        "g_glaT": fm8(np.asarray(inp["g_gla_norm"][0]).reshape(D)),
        "w_out": f(inp["w_out"][0]),
        "g_final_bc": f(np.tile(np.asarray(inp["g_final"]).reshape(1, D), (128, 1))),
        "ident": np.eye(128, dtype=np.float32),
        "maskT": np.triu(np.ones((128, 128), dtype=np.float32)),
        "sel": f(np.concatenate([np.zeros((NS, 128)), np.ones((1, 128))], axis=0)),
        "eye16": f(np.tile(np.eye(NS, dtype=np.float32).reshape(1, NS * NS), (128, 1))),
    }
    maps = []
    for c in range(8):
        rows = slice(NS * c, NS * (c + 1))
        m = dict(shared)
        m["xp"] = f(inp["x_prompt"][c])
        m["xs"] = f(np.asarray(inp["x_sample"])[rows, 0, :])
        m["cT"] = f(np.concatenate([np.asarray(inp["c_sample"])[rows], np.asarray(inp["c_prompt"])[c:c + 1]], axis=0).T)
        m["h0T"] = f(np.asarray(inp["state_lru_h"])[0, rows].T.reshape(8, 128, NS).transpose(1, 0, 2))
        m["c0T"] = f(np.asarray(inp["state_lru_conv"])[0, rows].transpose(2, 1, 0).reshape(8, 128, 3, NS).transpose(1, 0, 2, 3))
        m["S0"] = f(np.asarray(inp["state_gla"])[0, rows])
        maps.append(m)
    return maps


def kernel(**inputs):
    maps = _prep_inputs(inputs)
    nc = build_nc()
    res = run_bass_kernel_spmd(nc, maps, core_ids=list(range(8)))
    R = res.results
    y_prompt = np.stack([R[c]["yp"] for c in range(8)], axis=0).astype(np.float32)
    y_sample = np.concatenate([R[c]["ys"] for c in range(8)], axis=0)[:, None, :].astype(np.float32)
    hp = np.stack([R[c]["hpT"].T.reshape(D) for c in range(8)], axis=0)[None].astype(np.float32)
    cpo = np.stack([R[c]["cpT"].transpose(2, 1, 0).reshape(3, D) for c in range(8)], axis=0)[None].astype(np.float32)
    spo = np.stack([R[c]["sp"] for c in range(8)], axis=0)[None].astype(np.float32)
    hs = np.concatenate([R[c]["hsT"].transpose(2, 1, 0).reshape(NS, D) for c in range(8)], axis=0)[None].astype(np.float32)
    cso = np.concatenate([R[c]["csT"].transpose(3, 2, 1, 0).reshape(NS, 3, D) for c in range(8)], axis=0)[None].astype(np.float32)
    sso = np.concatenate([R[c]["ss"] for c in range(8)], axis=0)[None].astype(np.float32)
    return (y_prompt, y_sample, hp, cpo, spo, hs, cso, sso)
```
